# Optimizing a Trainium2 kernel written in Bass

```python
import jax, jax.numpy as jnp
from jax import lax
import numpy as np

D_MODEL = 1024
BATCH = 4
SEQ = 4096
DEPTH = 4
DEC_BATCH = 32
DEC_SEQ = 4
PAST_LEN = 8192
PAGE_SIZE = 128

N_MIXERS = 2
N_LAYERS_A = (DEPTH + 1) // 2
N_LAYERS_B = DEPTH // 2
HEAD_DIM = 64
N_SLOTS = 8
DIL_WINDOWS = (128, 512, 2048)
DIL_RATES = (1, 4, 16)
N_GROUPS_A = len(DIL_WINDOWS)
QKV_WIDTH = 3 * N_GROUPS_A * N_SLOTS * HEAD_DIM
A_WIDTH = N_SLOTS * HEAD_DIM
BAND_BLOCK = 128
CHUNK = 128
D_FFN_B = 6 * D_MODEL
D_V = D_FFN_B // 2
N_GROUPS_B = 8
GROUP_B = D_V // N_GROUPS_B
D_FF = 2816
RMS_EPS = 1e-6
LN_EPS = 1e-5

kernel_name = "dilated_window_gmlp_macaron_hybrid_step"


def rmsnorm(x, g):
    xf = x.astype(jnp.float32)
    y = xf * lax.rsqrt(jnp.mean(xf * xf, axis=-1, keepdims=True) + RMS_EPS)
    return (y * g.astype(jnp.float32)).astype(x.dtype)


def layernorm(x, g, b):
    xf = x.astype(jnp.float32)
    mu = jnp.mean(xf, axis=-1, keepdims=True)
    xc = xf - mu
    y = xc * lax.rsqrt(jnp.mean(xc * xc, axis=-1, keepdims=True) + LN_EPS)
    return (y * g.astype(jnp.float32) + b.astype(jnp.float32)).astype(x.dtype)


def swiglu(x, w_in, w_out):
    gate, up = jnp.split(x @ w_in, 2, axis=-1)
    return (jax.nn.silu(gate) * up) @ w_out


def project_qkv(h, w_qkv):
    b, s, _ = h.shape
    qkv = (h @ w_qkv).reshape(b, s, 3, N_GROUPS_A, N_SLOTS, HEAD_DIM)
    return qkv[:, :, 0], qkv[:, :, 1], qkv[:, :, 2]


def band_attention(q, k, v, dil, n_back):
    b, s, h, dh = q.shape
    s_sub = s // dil
    nb = -(-s_sub // BAND_BLOCK)
    pad = nb * BAND_BLOCK - s_sub

    def to_sub(a):
        a = a.reshape(b, s_sub, dil, h, dh).transpose(0, 2, 1, 3, 4)
        a = jnp.pad(a, ((0, 0), (0, 0), (0, pad), (0, 0), (0, 0)))
        return a.reshape(b, dil, nb, BAND_BLOCK, h, dh)

    def with_prev(a):
        prev = jnp.pad(a, ((0, 0), (0, 0), (1, 0), (0, 0), (0, 0), (0, 0)))[:, :, :nb]
        return jnp.concatenate([prev, a], axis=3)

    qs = to_sub(q)
    kb = with_prev(to_sub(k))
    vb = with_prev(to_sub(v))
    scores = jnp.einsum('brnqhd,brnkhd->brnhqk', qs, kb).astype(jnp.float32) * (HEAD_DIM ** -0.5)
    qi = jnp.arange(BAND_BLOCK)[:, None]
    ki = jnp.arange(2 * BAND_BLOCK)[None, :]
    dist = BAND_BLOCK + qi - ki
    band = (dist >= 0) & (dist <= n_back)
    key_sub = (jnp.arange(nb)[:, None] - 1) * BAND_BLOCK + ki
    mask = band[None] & (key_sub >= 0)[:, None, :]
    scores = jnp.where(mask[:, None], scores, -jnp.inf)
    m = jnp.max(scores, axis=-1, keepdims=True)
    p = jnp.exp(scores - m)
    denom = jnp.sum(p, axis=-1, keepdims=True)
    o = jnp.einsum('brnhqk,brnkhd->brnqhd', (p / denom).astype(v.dtype), vb).astype(jnp.float32)
    lse = (m + jnp.log(denom))[..., 0]
    o = o.reshape(b, dil, nb * BAND_BLOCK, h, dh)[:, :, :s_sub]
    o = o.transpose(0, 2, 1, 3, 4).reshape(b, s, h, dh)
    lse = lse.transpose(0, 1, 2, 4, 3).reshape(b, dil, nb * BAND_BLOCK, h)[:, :, :s_sub]
    lse = lse.transpose(0, 2, 1, 3).reshape(b, s, h)
    return o, lse


def cached_attention(q, k_new, v_new, k_buf, v_buf, dil, n_back):
    t = q.shape[1]
    buf_len = k_buf.shape[1]
    k_all = jnp.concatenate([k_buf, k_new], axis=1)
    v_all = jnp.concatenate([v_buf, v_new], axis=1)
    idx = buf_len + jnp.arange(t)[:, None] - dil * jnp.arange(n_back + 1)[None, :]
    valid = idx >= 0
    idx = jnp.maximum(idx, 0)
    kg = k_all[:, idx]
    vg = v_all[:, idx]
    scores = jnp.einsum('bthd,btjhd->bthj', q, kg).astype(jnp.float32) * (HEAD_DIM ** -0.5)
    scores = jnp.where(valid[None, :, None, :], scores, -jnp.inf)
    m = jnp.max(scores, axis=-1, keepdims=True)
    p = jnp.exp(scores - m)
    denom = jnp.sum(p, axis=-1, keepdims=True)
    o = jnp.einsum('bthj,btjhd->bthd', (p / denom).astype(v_new.dtype), vg).astype(jnp.float32)
    lse = (m + jnp.log(denom))[..., 0]
    return o, lse


def merge_groups(outs, lses, w_out, dtype):
    o = jnp.stack(outs, axis=2)
    wts = jax.nn.softmax(jnp.stack(lses, axis=2), axis=2)
    y = jnp.einsum('bsgh,bsghd->bshd', wts, o)
    b, s = y.shape[:2]
    return y.reshape(b, s, A_WIDTH).astype(dtype) @ w_out


def dilated_mixer_prompt(h, w_qkv, w_out):
    q, k, v = project_qkv(h, w_qkv)
    s = h.shape[1]
    outs, lses, rows = [], [], []
    for g in range(N_GROUPS_A):
        win, dil = DIL_WINDOWS[g], DIL_RATES[g]
        o, lse = band_attention(q[:, :, g], k[:, :, g], v[:, :, g], dil, win // dil)
        outs.append(o)
        lses.append(lse)
        keep = min(win, s)
        rows.append(jnp.stack([k[:, s - keep:, g], v[:, s - keep:, g]], axis=2))
    return merge_groups(outs, lses, w_out, h.dtype), rows


def dilated_mixer_sample(h, bufs, w_qkv, w_out):
    q, k, v = project_qkv(h, w_qkv)
    outs, lses, rows = [], [], []
    for g in range(N_GROUPS_A):
        win, dil = DIL_WINDOWS[g], DIL_RATES[g]
        buf = bufs[g]
        o, lse = cached_attention(q[:, :, g], k[:, :, g], v[:, :, g],
                                  buf[:, :, 0], buf[:, :, 1], dil, win // dil)
        outs.append(o)
        lses.append(lse)
        rows.append(jnp.stack([k[:, :, g], v[:, :, g]], axis=2))
    return merge_groups(outs, lses, w_out, h.dtype), rows


def chunk_gmlp(h, w_uv, ln_g, ln_b, w_s, b_s, w_out):
    b, s, _ = h.shape
    u, v = jnp.split(jax.nn.gelu(h @ w_uv), 2, axis=-1)
    v = layernorm(v, ln_g, ln_b)
    clen = min(s, CHUNK)
    n = s // clen
    causal = jnp.tril(jnp.ones((clen, clen), dtype=bool))
    w = jnp.where(causal[None], w_s[:, :clen, :clen], 0)
    vr = v.reshape(b, n, clen, N_GROUPS_B, GROUP_B)
    mixed = jnp.einsum('gts,bnsgc->bntgc', w, vr) + b_s[:, :clen].T[None, None, :, :, None]
    return (u * mixed.reshape(b, s, D_V)) @ w_out, v


def setup_inputs(seed: int = 0) -> dict:
    key = jax.random.key(seed)
    ks = jax.random.split(key, 24)
    f32 = jnp.float32

    def nrm(k, shape, scale=1.0):
        return jax.random.normal(k, shape, f32) * scale

    def gain(k, shape):
        return 1.0 + 0.02 * jax.random.normal(k, shape, f32)

    return {
        "x_prompt": nrm(ks[0], (BATCH, SEQ, D_MODEL)),
        "x_sample": nrm(ks[1], (DEC_BATCH, DEC_SEQ, D_MODEL)),
        "cache_kv_w128": nrm(ks[2], (N_LAYERS_A, DEC_BATCH, min(DIL_WINDOWS[0], PAST_LEN), 2, N_SLOTS, HEAD_DIM)),
        "cache_kv_w512": nrm(ks[3], (N_LAYERS_A, DEC_BATCH, min(DIL_WINDOWS[1], PAST_LEN), 2, N_SLOTS, HEAD_DIM)),
        "cache_kv_w2048": nrm(ks[4], (N_LAYERS_A, DEC_BATCH, min(DIL_WINDOWS[2], PAST_LEN), 2, N_SLOTS, HEAD_DIM)),
        "norm_ffn1": gain(ks[5], (DEPTH, D_MODEL)),
        "w_ffn1_in": nrm(ks[6], (DEPTH, D_MODEL, 2 * D_FF), D_MODEL ** -0.5),
        "w_ffn1_out": nrm(ks[7], (DEPTH, D_FF, D_MODEL), D_FF ** -0.5),
        "norm_mix": gain(ks[8], (DEPTH, D_MODEL)),
        "norm_ffn2": gain(ks[9], (DEPTH, D_MODEL)),
        "w_ffn2_in": nrm(ks[10], (DEPTH, D_MODEL, 2 * D_FF), D_MODEL ** -0.5),
        "w_ffn2_out": nrm(ks[11], (DEPTH, D_FF, D_MODEL), D_FF ** -0.5),
        "w_qkv_a": nrm(ks[12], (N_LAYERS_A, D_MODEL, QKV_WIDTH), D_MODEL ** -0.5),
        "w_out_a": nrm(ks[13], (N_LAYERS_A, A_WIDTH, D_MODEL), A_WIDTH ** -0.5),
        "w_uv_b": nrm(ks[14], (N_LAYERS_B, D_MODEL, 2 * D_V), D_MODEL ** -0.5),
        "ln_v_gain": gain(ks[15], (N_LAYERS_B, D_V)),
        "ln_v_bias": nrm(ks[16], (N_LAYERS_B, D_V), 0.02),
        "w_spatial": nrm(ks[17], (N_LAYERS_B, N_GROUPS_B, CHUNK, CHUNK), CHUNK ** -0.5),
        "b_spatial": gain(ks[18], (N_LAYERS_B, N_GROUPS_B, CHUNK)),
        "w_out_b": nrm(ks[19], (N_LAYERS_B, D_V, D_MODEL), D_V ** -0.5),
        "norm_final": gain(ks[20], (D_MODEL,)),
    }


def reference(x_prompt, x_sample, cache_kv_w128, cache_kv_w512, cache_kv_w2048,
              norm_ffn1, w_ffn1_in, w_ffn1_out, norm_mix, norm_ffn2, w_ffn2_in, w_ffn2_out,
              w_qkv_a, w_out_a, w_uv_b, ln_v_gain, ln_v_bias, w_spatial, b_spatial, w_out_b,
              norm_final):
    caches = (cache_kv_w128, cache_kv_w512, cache_kv_w2048)
    xp, xs = x_prompt, x_sample
    kv_prompt = [[] for _ in range(N_GROUPS_A)]
    kv_sample = [[] for _ in range(N_GROUPS_A)]
    v_rows = []
    for i in range(DEPTH):
        li = i // N_MIXERS
        xp = xp + 0.5 * swiglu(rmsnorm(xp, norm_ffn1[i]), w_ffn1_in[i], w_ffn1_out[i])
        xs = xs + 0.5 * swiglu(rmsnorm(xs, norm_ffn1[i]), w_ffn1_in[i], w_ffn1_out[i])
        hp = rmsnorm(xp, norm_mix[i])
        hs = rmsnorm(xs, norm_mix[i])
        if i % N_MIXERS == 0:
            mp, rows_p = dilated_mixer_prompt(hp, w_qkv_a[li], w_out_a[li])
            ms, rows_s = dilated_mixer_sample(hs, [c[li] for c in caches], w_qkv_a[li], w_out_a[li])
            for g in range(N_GROUPS_A):
                kv_prompt[g].append(rows_p[g])
                kv_sample[g].append(rows_s[g])
        else:
            mp, _ = chunk_gmlp(hp, w_uv_b[li], ln_v_gain[li], ln_v_bias[li],
                               w_spatial[li], b_spatial[li], w_out_b[li])
            ms, vs = chunk_gmlp(hs, w_uv_b[li], ln_v_gain[li], ln_v_bias[li],
                                w_spatial[li], b_spatial[li], w_out_b[li])
            v_rows.append(vs)
        xp = xp + mp
        xs = xs + ms
        xp = xp + 0.5 * swiglu(rmsnorm(xp, norm_ffn2[i]), w_ffn2_in[i], w_ffn2_out[i])
        xs = xs + 0.5 * swiglu(rmsnorm(xs, norm_ffn2[i]), w_ffn2_in[i], w_ffn2_out[i])
    y_prompt = rmsnorm(xp, norm_final)
    y_sample = rmsnorm(xs, norm_final)
    kv128_prompt = jnp.stack(kv_prompt[0])
    kv512_prompt = jnp.stack(kv_prompt[1])
    kv2048_prompt = jnp.stack(kv_prompt[2])
    kv128_sample = jnp.stack(kv_sample[0])
    kv512_sample = jnp.stack(kv_sample[1])
    kv2048_sample = jnp.stack(kv_sample[2])
    v_chunk_sample = jnp.stack(v_rows)
    return (y_prompt, y_sample, kv128_prompt, kv512_prompt, kv2048_prompt,
            kv128_sample, kv512_sample, kv2048_sample, v_chunk_sample)
```

```python
import numpy as np
import concourse.bass as bass
import concourse.mybir as mybir
from concourse.bass_utils import run_bass_kernel_spmd

F32 = mybir.dt.float32
BF16 = mybir.dt.bfloat16
AF = mybir.ActivationFunctionType
ALU = mybir.AluOpType
AX = mybir.AxisListType

D = 1024
KC = 8
TP = 2048
NS = 16
TT = TP + NS
DFF = 2816
NJ = 22
DEPTH = 4
QKVW = 4608
DV = 3072
SLOT = 8192
NSLOT = 3
RMS_EPS = 1e-6
LN_EPS = 1e-5

ENGS = ("pe", "act", "dve", "pool", "sp")


class Op:
    __slots__ = ("eng", "fn", "deps", "signal", "val", "dkey", "dinc", "idx")

    def __init__(self, eng, fn, dkey=None):
        self.eng = eng
        self.fn = fn
        self.deps = []
        self.signal = False
        self.val = None
        self.dkey = dkey
        self.dinc = 16


class Prog:
    def __init__(self):
        self.ops = []
        self.last_w = {}
        self.readers = {}
        self.last_eng = {}
        self.last_dma = {}
        self.bar_deps = []
        self.bar_pending = set()

    def barrier(self):
        self.bar_deps = list(self.last_eng.values()) + list(self.last_dma.values())
        self.bar_pending = set(ENGS)

    def add(self, eng, fn, reads=(), writes=(), dkey=None, dinc=16):
        op = Op(eng, fn, dkey)
        op.dinc = dinc
        deps = []
        ring_only = bool(writes) and all(isinstance(w, tuple) and w[0] == "w" for w in writes) and not reads
        if eng in self.bar_pending and not ring_only:
            self.bar_pending.discard(eng)
            deps.extend(self.bar_deps)
        for r in list(reads) + list(writes):
            lw = self.last_w.get(r)
            if lw is not None:
                deps.append(lw)
        for w in writes:
            deps.extend(self.readers.get(w, {}).values())
        seen = set()
        for d in deps:
            if id(d) in seen or d is op:
                continue
            seen.add(id(d))
            if d.eng == "pe" and eng == "pe" and d.dkey is None and dkey is None:
                continue
            op.deps.append(d)
            d.signal = True
        rkey = dkey if dkey is not None else eng
        for r in reads:
            self.readers.setdefault(r, {})[rkey] = op
        for w in writes:
            self.last_w[w] = op
            self.readers[w] = {}
        op.idx = len(self.ops)
        self.ops.append(op)
        if dkey is not None:
            self.last_dma[dkey] = op
        else:
            self.last_eng[eng] = op
        return op

    def emit(self, nc, block, sems, dsems):
        cnt = {e: 0 for e in ENGS}
        dcnt = {}
        for op in self.ops:
            if op.dkey is not None:
                dcnt[op.dkey] = dcnt.get(op.dkey, 0) + op.dinc
                op.val = dcnt[op.dkey]
            elif op.signal:
                cnt[op.eng] += 1
                op.val = cnt[op.eng]
        for op in self.ops:
            if op.dkey == "const":
                op.val = dcnt["const"]
        per_eng = {e: [o for o in self.ops if o.eng == e] for e in ENGS}

        def run(engname, h):
            waited = {}
            for op in per_eng[engname]:
                need = {}
                for d in op.deps:
                    key = ("d", d.dkey) if d.dkey is not None else ("e", d.eng)
                    if d.val > need.get(key, 0):
                        need[key] = d.val
                for key, v in need.items():
                    if waited.get(key, 0) >= v:
                        continue
                    sem = dsems[key[1]] if key[0] == "d" else sems[key[1]]
                    h.wait_ge(sem, v)
                    waited[key] = v
                ins = op.fn(h)
                if op.dkey is not None:
                    if op.dinc == 1:
                        ins.then_inc(dsems[op.dkey])
                    else:
                        ins.then_inc(dsems[op.dkey], op.dinc)
                elif op.signal:
                    ins.then_inc(sems[op.eng], 1)
            for op in per_eng[engname]:
                pass

        return run


def build_nc(cfg):
    depth = cfg.get("depth", DEPTH)
    nc = bass.Bass("TRN2", target_bir_lowering=False)
    P = Prog()

    def din(name, shape, dt=F32):
        return nc.dram_tensor(name, list(shape), dt, kind="ExternalInput").ap()

    def dout(name, shape, dt=F32):
        return nc.dram_tensor(name, list(shape), dt, kind="ExternalOutput").ap()

    xin = din("xin", [TT, D])
    ident_d = din("ident", [128, 128])
    norm_ffn1 = din("norm_ffn1", [DEPTH, D])
    norm_ffn2 = din("norm_ffn2", [DEPTH, D])
    norm_mix = din("norm_mix", [DEPTH, D])
    norm_final = din("norm_final", [1, D])
    w_ffn1_in = din("w_ffn1_in", [DEPTH, D, 2 * DFF])
    w_ffn1_out = din("w_ffn1_out", [DEPTH, DFF, D])
    w_ffn2_in = din("w_ffn2_in", [DEPTH, D, 2 * DFF])
    w_ffn2_out = din("w_ffn2_out", [DEPTH, DFF, D])
    yout = dout("yout", [TT, D])
    w_qkv_a = din("w_qkv_a", [2, D, QKVW])
    w_out_a = din("w_out_a", [2, 512, D])
    cache128 = din("cache128", [2, 4, 128, 1024])
    cache512 = din("cache512", [2, 4, 512, 1024])
    cache2048 = din("cache2048", [2, 4, 2048, 1024])
    maskN_d = din("maskN", [128, 512])
    maskH_d = din("maskH", [128, 512])
    onesel_d = din("onesel", [128, 256])
    bdm_d = din("bdm", [8, 512])
    mbc_d = din("mbc", [128, 4])
    mbn_d = din("mbn", [NS, 48])
    w_uv_b = din("w_uv_b", [2, D, 2 * DV])
    w_out_b = din("w_out_b", [2, DV, D])
    ln_g_d = din("ln_v_gain", [2, DV])
    ln_b_d = din("ln_v_bias", [2, DV])
    wspT_d = din("wspT", [2, 8, 128, 128])
    wsmT_d = din("wsmT", [2, 8, NS, NS])
    bsp_d = din("b_spatial", [2, 8, 128])
    bsm_d = din("bsm", [2, 8, NS])
    trilT_d = din("trilT", [128, 128])
    trilS_d = din("trilS", [NS, NS])
    kvp128 = dout("kvp128", [2, 128, 2, 512])
    kvp512 = dout("kvp512", [2, 512, 2, 512])
    kvp2048 = dout("kvp2048", [2, 2048, 2, 512])
    kvs = dout("kvs", [2, 3, NS, 2, 512])
    vchunk = dout("vchunk", [2, NS, DV])
    kvx_ins = [nc.dram_tensor(f"kvx_in{i}", [3072, 2048], BF16).ap() for i in range(2)]
    kvx_outs = [[nc.dram_tensor(f"kvx_out{i}_{c}", [512, 2048], BF16).ap() for c in range(12)] for i in range(2)]

    import contextlib
    es = contextlib.ExitStack()
    with es:
        def sb(name, shape, dt):
            return es.enter_context(nc.sbuf_tensor(name, list(shape), dt))

        xT = sb("xT", [128, KC, TT], F32)
        wring = sb("wring", [128, NSLOT, SLOT], BF16)
        ident = sb("ident_sb", [128, 128], F32)
        ones_b = sb("ones_b", [128, 128], BF16)
        gam = sb("gam", [128, 3 * DEPTH + 1, KC], F32)
        epsb = sb("epsb", [128, 1], F32)
        ARENA_WORDS = 24000
        arena = sb("arena", [128, ARENA_WORDS], F32)
        ar = {"off": 0}

        def phase_begin():
            P.barrier()
            ar["off"] = 0

        def alloc(shape, dt):
            nel = int(np.prod(shape))
            words = (nel * (2 if dt == BF16 else 4) + 3) // 4
            o = ar["off"]
            assert o + words <= ARENA_WORDS, (o, words)
            ar["off"] = o + words
            v = arena[:, o:o + words]
            if dt == BF16:
                v = v.bitcast(BF16)[:, :nel]
            if len(shape) == 2:
                v = v.rearrange("p (a b) -> p a b", a=shape[0])
            elif len(shape) == 3:
                v = v.rearrange("p (a b c) -> p a b c", a=shape[0], b=shape[1])
            return v
        banks = [es.enter_context(nc.psum_tensor(f"ps{b}", [128, 512], F32)) for b in range(8)]

        sems = {e: es.enter_context(nc.semaphore(f"s_{e}")) for e in ENGS}
        dsems = {}

        state = {"bank": 0, "slot": 0, "tmp": 0, "rstd": 0}

        def new_bank():
            b = state["bank"]
            state["bank"] = (b + 1) % 8
            return b

        def new_slot():
            s = state["slot"]
            state["slot"] = (s + 1) % NSLOT
            return s

        P.add("sp", lambda h: h.dma_start(out=ident[:], in_=ident_d[:, :]), writes=["ident"], dkey="const")
        for i, nd in enumerate((norm_ffn1, norm_mix, norm_ffn2)):
            for L in range(DEPTH):
                P.add("sp", lambda h, nd=nd, L=L, i=i: h.dma_start(
                    out=gam[:, i * DEPTH + L, :], in_=nd[L, :].rearrange("(kc p) -> p kc", p=128),
                    allow_slow_non_contiguous=True), writes=[("gam", i * DEPTH + L)], dkey="const")
        P.add("sp", lambda h: h.dma_start(
            out=gam[:, 3 * DEPTH, :], in_=norm_final[0, :].rearrange("(kc p) -> p kc", p=128),
            allow_slow_non_contiguous=True), writes=[("gam", 3 * DEPTH)], dkey="const")
        P.add("dve", lambda h: h.memset(ones_b[:], 1.0), writes=["ones"])

        TILES = [(t, 512 * t, 512) for t in range(4)] + [("s", TP, NS)]
        HALF_TILES = [[TILES[0], TILES[1], TILES[4]], [TILES[2], TILES[3]]]

        def lcol(ti):
            return 512 * ti

        def load_x(xs):
            blocks = [(128 * i, 128) for i in range(16)] + [(TP, NS)]
            for bi, (r0, n) in enumerate(blocks):
                sl = bi % 2
                tname = ("s" if r0 >= TP else r0 // 512)
                P.add("sp", lambda h, r0=r0, n=n, sl=sl: h.dma_start(out=xs[:n, sl, :], in_=xin[r0:r0 + n, :]),
                      writes=[("xs", sl)], dkey=f"xs{sl}")
                for q in range(2):
                    b = new_bank()
                    for k4 in range(4):
                        kc = 4 * q + k4
                        P.add("pe", lambda h, b=b, k4=k4, kc=kc, n=n, sl=sl: h.transpose(
                            out=banks[b][:, k4 * 128:k4 * 128 + n], in_=xs[:n, sl, kc * 128:(kc + 1) * 128],
                            identity=ident[:n, :n]),
                            reads=[("xs", sl), "ident"], writes=[("ps", b)])
                    P.add("act" if q == 0 else "dve",
                          (lambda h, b=b, q=q, r0=r0, n=n: h.activation(
                              out=xT[:, 4 * q:4 * q + 4, r0:r0 + n],
                              in_=banks[b][:, :].rearrange("p (k c) -> p k c", k=4)[:, :, :n], func=AF.Copy))
                          if q == 0 else
                          (lambda h, b=b, q=q, r0=r0, n=n: h.tensor_copy(
                              out=xT[:, 4 * q:4 * q + 4, r0:r0 + n],
                              in_=banks[b][:, :].rearrange("p (k c) -> p k c", k=4)[:, :, :n])),
                          reads=[("ps", b)], writes=[("xT", kc, tname, (r0 % 512) // 128) for kc in range(4 * q, 4 * q + 4)])

        def xT_res(kc, tname):
            if tname == "s":
                return [("xT", kc, "s", 0)]
            return [("xT", kc, tname, i) for i in range(4)]

        def rmsnorm(tile, gidx, out_fn, out_res, sq, rstd, sqres="sq", rstdres=None):
            tname, x0, n = tile
            rs = state["rstd"]
            state["rstd"] = (rs + 1) % 2
            rres = ("rstd", rs) if rstdres is None else rstdres
            P.add("act", lambda h: h.activation(out=sq[:, :, :n], in_=xT[:, :, x0:x0 + n], func=AF.Square),
                  reads=[r for kc in range(KC) for r in xT_res(kc, tname)], writes=[sqres])
            b = new_bank()
            for kc in range(KC):
                P.add("pe", lambda h, kc=kc: h.matmul(banks[b][:, :n], lhsT=ones_b[:], rhs=sq[:, kc, :n],
                                                      start=(kc == 0), stop=(kc == KC - 1)),
                      reads=[sqres, "ones"], writes=[("ps", b)])
            P.add("act", lambda h: h.activation(out=rstd[:, rs, :n], in_=banks[b][:, :n], func=AF.Sqrt,
                                                scale=1.0 / D, bias=epsb[:]),
                  reads=[("ps", b), "epsb"], writes=[rres])
            P.add("dve", lambda h: h.reciprocal(out=rstd[:, rs, :n], in_=rstd[:, rs, :n]),
                  reads=[rres], writes=[rres])
            for kc in range(KC):
                P.add("dve", lambda h, kc=kc: h.scalar_tensor_tensor(
                    out=out_fn(kc), in0=xT[:, kc, x0:x0 + n], scalar=gam[:, gidx, kc:kc + 1],
                    in1=rstd[:, rs, :n], op0=ALU.mult, op1=ALU.mult),
                    reads=xT_res(kc, tname) + [rres, ("gam", gidx)], writes=[out_res(kc)])

        P.add("dve", lambda h: h.memset(epsb[:], RMS_EPS), writes=["epsb"])

        def ffn(L, which, hf, xn, act, sq, rstd, tmp):
            tiles = HALF_TILES[hf]
            w_in = (w_ffn1_in if which == 1 else w_ffn2_in)[L].rearrange("(kc p) n -> p kc n", p=128)
            w_out = (w_ffn1_out if which == 1 else w_ffn2_out)[L].rearrange("(kc p) n -> p kc n", p=128)
            gidx = (0 if which == 1 else 2) * DEPTH + L
            for ti, tile in enumerate(tiles):
                n = tile[2]
                rmsnorm(tile, gidx, lambda kc, ti=ti, n=n: xn[:, kc, lcol(ti):lcol(ti) + n],
                        lambda kc, ti=ti: ("xn", kc, ti), sq, rstd)
            for j0 in range(0, NJ, 4):
                jb = min(4, NJ - j0)
                s = new_slot()
                wv = wring[:, s, :2 * KC * jb * 128].rearrange("p (g k n) -> p g k n", g=2, k=KC)
                for g in range(2):
                    c0 = g * DFF + j0 * 128
                    P.add("pool", lambda h, g=g, c0=c0, wv=wv, jb=jb: h.dma_start(
                        out=wv[:, g, :, :], in_=w_in[:, :, c0:c0 + jb * 128]),
                        writes=[("w", s, g)], dkey=f"w{s}_{g}")
                for ti, tile in enumerate(tiles):
                    n = tile[2]
                    l0 = lcol(ti)
                    for jj in range(jb):
                        bg = new_bank()
                        bu = new_bank()
                        for g, b in ((0, bg), (1, bu)):
                            for kc in range(KC):
                                P.add("pe", lambda h, g=g, b=b, kc=kc, jj=jj, wv=wv, l0=l0, n=n: h.matmul(
                                    banks[b][:, :n], lhsT=wv[:, g, kc, jj * 128:(jj + 1) * 128],
                                    rhs=xn[:, kc, l0:l0 + n], start=(kc == 0), stop=(kc == KC - 1)),
                                    reads=[("w", s, g), ("xn", kc, ti)], writes=[("ps", b)])
                        tb = state["tmp"]
                        state["tmp"] = (tb + 1) % 3
                        P.add("act", lambda h, bg=bg, tb=tb, n=n: h.activation(
                            out=tmp[:, tb, :n], in_=banks[bg][:, :n], func=AF.Silu),
                            reads=[("ps", bg)], writes=[("tmp", tb)])
                        P.add("dve", lambda h, bu=bu, tb=tb, n=n, j=j0 + jj, l0=l0: h.tensor_tensor(
                            out=act[:, j, l0:l0 + n], in0=tmp[:, tb, :n], in1=banks[bu][:, :n], op=ALU.mult),
                            reads=[("tmp", tb), ("ps", bu)], writes=[("act", j0 + jj, ti)])
            for n0 in range(0, KC, 2):
                s = new_slot()
                wv = wring[:, s, :NJ * 256].rearrange("p (k n) -> p k n", k=NJ)
                P.add("pool", lambda h, wv=wv, n0=n0: h.dma_start(
                    out=wv[:, :, :], in_=w_out[:, :, n0 * 128:(n0 + 2) * 128]),
                    writes=[("w", s, 0), ("w", s, 1)], dkey=f"w{s}_0")
                for ti, tile in enumerate(tiles):
                    tname, x0, n = tile
                    l0 = lcol(ti)
                    for nn in range(2):
                        b = new_bank()
                        for kc in range(NJ):
                            P.add("pe", lambda h, b=b, kc=kc, nn=nn, wv=wv, l0=l0, n=n: h.matmul(
                                banks[b][:, :n], lhsT=wv[:, kc, nn * 128:(nn + 1) * 128],
                                rhs=act[:, kc, l0:l0 + n], start=(kc == 0), stop=(kc == NJ - 1)),
                                reads=[("w", s, 0), ("w", s, 1), ("act", kc, ti)], writes=[("ps", b)])
                        kco = n0 + nn
                        P.add("dve", lambda h, b=b, kco=kco, x0=x0, n=n: h.scalar_tensor_tensor(
                            out=xT[:, kco, x0:x0 + n], in0=banks[b][:, :n], scalar=0.5,
                            in1=xT[:, kco, x0:x0 + n], op0=ALU.mult, op1=ALU.add),
                            reads=[("ps", b)] + xT_res(kco, tname), writes=xT_res(kco, tname))

        def final_out(xs, yT, sq, rstd):
            gidx = 3 * DEPTH
            for tile in TILES:
                tname, x0, n = tile
                rmsnorm(tile, gidx, lambda kc, n=n: yT[:, kc, :n], lambda kc: ("yT", kc), sq, rstd)
                for sub in range(0, n, 128):
                    m = min(128, n - sub)
                    sl = state["xs_out"]
                    state["xs_out"] = (sl + 1) % 2
                    for q in range(2):
                        b = new_bank()
                        for k4 in range(4):
                            kc = 4 * q + k4
                            P.add("pe", lambda h, b=b, k4=k4, kc=kc, sub=sub, m=m: h.transpose(
                                out=banks[b][:m, k4 * 128:(k4 + 1) * 128], in_=yT[:, kc, sub:sub + m],
                                identity=ident[:, :]),
                                reads=[("yT", kc), "ident"], writes=[("ps", b)])
                        P.add("act" if q == 0 else "dve",
                              (lambda h, b=b, q=q, sl=sl, m=m: h.activation(
                                  out=xs[:m, sl, 512 * q:512 * q + 512], in_=banks[b][:m, :], func=AF.Copy))
                              if q == 0 else
                              (lambda h, b=b, q=q, sl=sl, m=m: h.tensor_copy(
                                  out=xs[:m, sl, 512 * q:512 * q + 512], in_=banks[b][:m, :])),
                              reads=[("ps", b)], writes=[("xs", sl, q)])
                    r0 = x0 + sub
                    P.add("sp", lambda h, sl=sl, m=m, r0=r0: h.dma_start(out=yout[r0:r0 + m, :], in_=xs[:m, sl, :]),
                          reads=[("xs", sl, 0), ("xs", sl, 1)], writes=[("yo", sl)], dkey=f"yo{sl}")


        DIL = (1, 4, 16)

        def blk_cols(g, blk):
            d = DIL[g]
            npr = 16 // d
            r, nb = blk // npr, blk % npr
            st = r + 128 * nb * d
            return st, d

        def out_blocks(g):
            d = DIL[g]
            npr = 16 // d
            return [r * npr + npr - 1 for r in range(d)]

        def mixer_a(L):
            li = L // 2
            wqkv = w_qkv_a[li].rearrange("(kc p) n -> p kc n", p=128)
            kvx_in, kvx_out = kvx_ins[li], kvx_outs[li]
            kvp = (kvp128, kvp512, kvp2048)
            caches = (cache128, cache512, cache2048)
            phase_begin()
            hT = alloc([KC, TT], BF16)
            ysT = alloc([8, NS], BF16)
            maskN = alloc([512], BF16)
            maskH = alloc([512], BF16)
            P.add("pool", lambda h: h.dma_start(out=maskN[:, :], in_=maskN_d[:, :]), writes=["masks"], dkey="mk")
            P.add("pool", lambda h: h.dma_start(out=maskH[:, :], in_=maskH_d[:, :]), writes=["masks"], dkey="mk")
            markY = ar["off"]
            skv = alloc([3, 1536], F32)
            mark1 = ar["off"]
            kst = alloc([2, 2048], BF16)
            vst = alloc([16, 512], BF16)
            fst = alloc([2, 512], F32)
            sq = alloc([KC, 512], BF16)
            rstd = alloc([2, 512], F32)
            for tile in TILES:
                tname, x0, n = tile
                rmsnorm(tile, DEPTH + L, lambda kc, x0=x0, n=n: hT[:, kc, x0:x0 + n],
                        lambda kc, tname=tname: ("hT", kc, tname), sq, rstd)
            fi = [0]
            for g in range(3):
                d = DIL[g]
                npr = 16 // d
                sA = new_slot()
                wkv = wring[:, sA, :].rearrange("p (a k n) -> p a k n", a=2, k=KC)
                for a in range(2):
                    c0 = ((a + 1) * 3 + g) * 512
                    P.add("pool", lambda h, a=a, c0=c0, wkv=wkv: h.dma_start(out=wkv[:, a, :, :], in_=wqkv[:, :, c0:c0 + 512]),
                          writes=[("w", sA, a)], dkey=f"w{sA}_{a}")
                sB = new_slot()
                wqv = wring[:, sB, :KC * 512].rearrange("p (k n) -> p k n", k=KC)
                P.add("pool", lambda h, g=g, wqv=wqv: h.dma_start(out=wqv[:, :, :], in_=wqkv[:, :, g * 512:(g + 1) * 512]),
                      writes=[("w", sB, 0), ("w", sB, 1)], dkey=f"w{sB}_0")
                for hp in range(4):
                    ks = hp % 2
                    u = g * 4 + hp
                    for tt in range(4):
                        b = new_bank()
                        for kc in range(KC):
                            P.add("pe", lambda h, b=b, kc=kc, hp=hp, tt=tt, wkv=wkv: h.matmul(
                                banks[b][:, :], lhsT=wkv[:, 0, kc, hp * 128:(hp + 1) * 128],
                                rhs=hT[:, kc, 512 * tt:512 * tt + 512], start=(kc == 0), stop=(kc == KC - 1)),
                                reads=[("w", sA, 0), ("hT", kc, tt)], writes=[("ps", b)])
                        w_ = 512 // d
                        P.add("act", lambda h, b=b, ks=ks, tt=tt, d=d, w_=w_: h.activation(
                            out=kst[:, ks, :].rearrange("p (r m) -> p r m", r=d)[:, :, w_ * tt:w_ * (tt + 1)],
                            in_=banks[b][:, :].rearrange("p (m r) -> p r m", r=d), func=AF.Copy),
                            reads=[("ps", b)], writes=[("kst", ks, tt)])
                    P.add("sp", lambda h, ks=ks, u=u: h.dma_start(out=kvx_in[u * 128:(u + 1) * 128, :], in_=kst[:, ks, :]),
                          reads=[("kst", ks, tt) for tt in range(4)], writes=[("kvx_in", "k", u)], dkey=f"kst{ks}")
                oblks = out_blocks(g)
                for blk in range(16):
                    st, stp = blk_cols(g, blk)
                    b = new_bank()
                    for kc in range(KC):
                        P.add("pe", lambda h, b=b, kc=kc, st=st, stp=stp, wkv=wkv: h.matmul(
                            banks[b][:, :], lhsT=hT[:, kc, st:st + 127 * stp + 1:stp], rhs=wkv[:, 1, kc, :],
                            start=(kc == 0), stop=(kc == KC - 1)),
                            reads=[("w", sA, 1)] + [("hT", kc, t) for t in range(4)], writes=[("ps", b)])
                    P.add("act", lambda h, b=b, blk=blk: h.activation(out=vst[:, blk, :], in_=banks[b][:, :], func=AF.Copy),
                          reads=[("ps", b)], writes=[("vst", blk)])
                    if blk in oblks:
                        row0 = oblks.index(blk)
                        f = fi[0] % 2
                        fi[0] += 1
                        P.add("dve", lambda h, b=b, f=f: h.tensor_copy(out=fst[:, f, :], in_=banks[b][:, :]),
                              reads=[("ps", b), ("vst", blk)], writes=[("fst", f)])
                        P.add("sp", lambda h, f=f, g=g, row0=row0, d=d: h.dma_start(
                            out=kvp[g][li, row0:row0 + 127 * d + 1:d, 1, :], in_=fst[:, f, :]),
                            reads=[("fst", f)], writes=[("kvp", g, blk, 1)], dkey=f"fst{f}")
                        b2 = new_bank()
                        for kc in range(KC):
                            P.add("pe", lambda h, b2=b2, kc=kc, st=st, stp=stp, wkv=wkv: h.matmul(
                                banks[b2][:, :], lhsT=hT[:, kc, st:st + 127 * stp + 1:stp], rhs=wkv[:, 0, kc, :],
                                start=(kc == 0), stop=(kc == KC - 1)),
                                reads=[("w", sA, 0)] + [("hT", kc, t) for t in range(4)], writes=[("ps", b2)])
                        f = fi[0] % 2
                        fi[0] += 1
                        P.add("dve", lambda h, b2=b2, f=f: h.tensor_copy(out=fst[:, f, :], in_=banks[b2][:, :]),
                              reads=[("ps", b2)], writes=[("fst", f)])
                        P.add("sp", lambda h, f=f, g=g, row0=row0, d=d: h.dma_start(
                            out=kvp[g][li, row0:row0 + 127 * d + 1:d, 0, :], in_=fst[:, f, :]),
                            reads=[("fst", f)], writes=[("kvp", g, blk, 0)], dkey=f"fst{f}")
                vreg = kvx_in[1536 + g * 512:1536 + (g + 1) * 512, :].rearrange("a (b f) -> (a b) f", f=512)
                P.add("sp", lambda h, vreg=vreg: h.dma_start(
                    out=vreg.rearrange("(blk p) f -> p blk f", p=128), in_=vst[:, :, :]),
                    reads=[("vst", blk) for blk in range(16)], writes=[("kvx_in", "v", g)], dkey="vst")
                for idx, (wv_, a_, res) in enumerate(((wqv, None, ("w", sB, 0)), (wkv, 0, ("w", sA, 0)), (wkv, 1, ("w", sA, 1)))):
                    b = new_bank()
                    for kc in range(KC):
                        P.add("pe", lambda h, b=b, kc=kc, wv_=wv_, a_=a_: h.matmul(
                            banks[b][:NS, :], lhsT=hT[:, kc, TP:TP + NS],
                            rhs=(wv_[:, kc, :] if a_ is None else wv_[:, a_, kc, :]),
                            start=(kc == 0), stop=(kc == KC - 1)),
                            reads=[res, ("hT", kc, "s")], writes=[("ps", b)])
                    P.add("dve", lambda h, b=b, idx=idx, g=g: h.tensor_copy(
                        out=skv[:NS, idx, g * 512:(g + 1) * 512], in_=banks[b][:NS, :]),
                        reads=[("ps", b)], writes=[("skv", idx, g)])
                for idx in (1, 2):
                    P.add("sp", lambda h, idx=idx, g=g: h.dma_start(
                        out=kvs[li, g, :, idx - 1, :], in_=skv[:NS, idx, g * 512:(g + 1) * 512]),
                        reads=[("skv", idx, g)], writes=[("kvs", g, idx)], dkey="kvs")
            if cfg.get("a_stop", 9) <= 1:
                return
            for c in range(12):
                rd = [("kvx_in", "k", 2 * c), ("kvx_in", "k", 2 * c + 1)] if c < 6 else [("kvx_in", "v", (c - 6) // 2)]
                P.add("pool", lambda h, c=c: h.collective_compute(
                    "AllGather", ALU.bypass, replica_groups=[[0, 1], [2, 3], [4, 5], [6, 7]],
                    ins=[kvx_in[256 * c:256 * (c + 1), :].opt()], outs=[kvx_out[c].opt()]),
                    reads=rd, writes=[("kvx_out", c)], dkey=f"cc{li}_{c}", dinc=1)
            P.barrier()
            ar["off"] = mark1
            if cfg.get("a_stop", 9) <= 2:
                return
            sample_attention(li, skv, ysT, caches)
            P.barrier()
            ar["off"] = markY
            if cfg.get("a_stop", 9) <= 3:
                return
            qT = alloc([2048], BF16)
            kTo = alloc([2048], BF16)
            kTh = alloc([2048], BF16)
            vo = alloc([16, 128], BF16)
            vh = alloc([16, 128], BF16)
            acc = alloc([4, 2048], F32)
            pT = alloc([2, 512], BF16)
            yb = alloc([2, 2, 512], BF16)
            so = new_slot()
            wo = wring[:, so, :].rearrange("p (h n) -> p h n", h=8)
            P.add("pool", lambda h: h.dma_start(out=wo[:64, :, :], in_=w_out_a[li].rearrange("(h d) n -> d h n", d=64)),
                  writes=[("w", so, 0), ("w", so, 1)], dkey=f"w{so}_0")
            pi = [0]
            for hp in range(4):
                for g in range(3):
                    d = DIL[g]
                    npr = 16 // d
                    u = g * 4 + hp
                    sq_ = new_slot()
                    while sq_ == so:
                        sq_ = new_slot()
                    wq1 = wring[:, sq_, :KC * 128].rearrange("p (k n) -> p k n", k=KC)
                    P.add("pool", lambda h, wq1=wq1, g=g, hp=hp: h.dma_start(
                        out=wq1[:, :, :], in_=wqkv[:, :, g * 512 + hp * 128:g * 512 + (hp + 1) * 128]),
                        writes=[("w", sq_, 0), ("w", sq_, 1)], dkey=f"w{sq_}_0")
                    P.add("sp", lambda h, u=u: h.dma_start(out=kTo[:, :], in_=kvx_in[u * 128:(u + 1) * 128, :]),
                          reads=[("kvx_in", "k", u)], writes=["kTo"], dkey="kTo")
                    P.add("sp", lambda h, u=u, d=d, npr=npr: h.dma_start(
                        out=kTh[:, :d * 128].rearrange("p (r c) -> p r c", r=d),
                        in_=kvx_out[u // 2][(u % 2) * 128:(u % 2) * 128 + 128, :].rearrange(
                            "p (r nb c) -> p r nb c", r=d, nb=npr)[:, :, npr - 1, :]),
                        reads=[("kvx_out", u // 2)], writes=["kTh"], dkey="kTh")
                    vreg_i = kvx_in[1536 + g * 512:1536 + (g + 1) * 512, :].rearrange("a (b f) -> (a b) f", f=512)
                    P.add("sp", lambda h, vreg_i=vreg_i, hp=hp: h.dma_start(
                        out=vo[:, :, :], in_=vreg_i.rearrange("(blk p) f -> p blk f", p=128)[:, :, hp * 128:(hp + 1) * 128]),
                        reads=[("kvx_in", "v", g)], writes=["vo"], dkey="vo")
                    for c2 in range(2):
                        cch = 6 + 2 * g + c2
                        vview = kvx_out[cch][0:256, :].rearrange("a (b f) -> (a b) f", f=512).rearrange(
                            "(blk p) f -> p blk f", p=128)
                        if npr <= 8:
                            cntb = 8 // npr
                            srcv = vview[:, npr - 1:8:npr, hp * 128:(hp + 1) * 128]
                            dstv = vh[:, c2 * cntb:(c2 + 1) * cntb, :]
                        else:
                            if c2 == 0:
                                continue
                            srcv = vview[:, 7:8, hp * 128:(hp + 1) * 128]
                            dstv = vh[:, 0:1, :]
                        P.add("sp", lambda h, srcv=srcv, dstv=dstv: h.dma_start(out=dstv, in_=srcv),
                              reads=[("kvx_out", cch)], writes=["vh"], dkey=f"vh{c2}")
                    for tt in range(4):
                        b = new_bank()
                        for kc in range(KC):
                            P.add("pe", lambda h, b=b, kc=kc, tt=tt, wq1=wq1: h.matmul(
                                banks[b][:, :], lhsT=wq1[:, kc, :], rhs=hT[:, kc, 512 * tt:512 * tt + 512],
                                start=(kc == 0), stop=(kc == KC - 1)),
                                reads=[("w", sq_, 0), ("hT", kc, tt)], writes=[("ps", b)])
                        w_ = 512 // d
                        P.add("act", lambda h, b=b, tt=tt, d=d, w_=w_: h.activation(
                            out=qT[:, :].rearrange("p (r m) -> p r m", r=d)[:, :, w_ * tt:w_ * (tt + 1)],
                            in_=banks[b][:, :].rearrange("p (m r) -> p r m", r=d), func=AF.Copy, scale=0.125),
                            reads=[("ps", b)], writes=[("qT", tt)])
                    for blk in range(16):
                        r, nb = blk // npr, blk % npr
                        c0 = blk * 128
                        pp = pi[0] % 2
                        pi[0] += 1
                        mk = maskH if nb == 0 else maskN
                        for a in range(2):
                            bs = new_bank()
                            pa = slice(64 * a, 64 * a + 64)
                            if nb == 0:
                                kprev = kTh[pa, r * 128:(r + 1) * 128]
                                kres = "kTh"
                            else:
                                kprev = kTo[pa, c0 - 128:c0]
                                kres = "kTo"
                            P.add("pe", lambda h, bs=bs, pa=pa, kprev=kprev, c0=c0: h.matmul(
                                banks[bs][:, 0:128], lhsT=kprev, rhs=qT[pa, c0:c0 + 128], start=True, stop=True),
                                reads=[kres] + [("qT", t) for t in range(4)], writes=[("ps", bs)])
                            P.add("pe", lambda h, bs=bs, pa=pa, c0=c0: h.matmul(
                                banks[bs][:, 128:256], lhsT=kTo[pa, c0:c0 + 128], rhs=qT[pa, c0:c0 + 128],
                                start=True, stop=True),
                                reads=["kTo"] + [("qT", t) for t in range(4)], writes=[("ps", bs)])
                            P.add("act", lambda h, bs=bs, pp=pp, a=a: h.activation(
                                out=pT[:, pp, 256 * a:256 * a + 256], in_=banks[bs][:, 0:256], func=AF.Exp),
                                reads=[("ps", bs)], writes=[("pT", pp, a)])
                            P.add("dve", lambda h, pp=pp, mk=mk, a=a: h.tensor_tensor(
                                out=pT[:, pp, 256 * a:256 * a + 256], in0=pT[:, pp, 256 * a:256 * a + 256],
                                in1=mk[:, 0:256], op=ALU.mult),
                                reads=[("pT", pp, a), "masks"], writes=[("pT", pp, a)])
                        bo = new_bank()
                        for a in range(2):
                            vprev = vh[:, r, 64 * a:64 * a + 64] if nb == 0 else vo[:, blk - 1, 64 * a:64 * a + 64]
                            vres = "vh" if nb == 0 else "vo"
                            for c, lh, rs_ in ((0, vprev, vres), (1, vo[:, blk, 64 * a:64 * a + 64], "vo")):
                                P.add("pe", lambda h, bo=bo, a=a, c=c, lh=lh, pp=pp: h.matmul(
                                    banks[bo][:64, (2 * a) * 128:(2 * a + 1) * 128], lhsT=lh,
                                    rhs=pT[:, pp, (2 * a + c) * 128:(2 * a + c + 1) * 128], start=(c == 0), stop=(c == 1)),
                                    reads=[rs_, ("pT", pp, a)], writes=[("ps", bo)])
                            for c in range(2):
                                P.add("pe", lambda h, bo=bo, a=a, c=c, pp=pp: h.matmul(
                                    banks[bo][:64, (2 * a + 1) * 128:(2 * a + 2) * 128], lhsT=ones_b[:, :64],
                                    rhs=pT[:, pp, (2 * a + c) * 128:(2 * a + c + 1) * 128], start=(c == 0), stop=(c == 1)),
                                    reads=["ones", ("pT", pp, a)], writes=[("ps", bo)])
                        st, stp = blk_cols(g, blk)
                        accv = acc[:64, :, st:st + 127 * stp + 1:stp]
                        src = banks[bo][:64, :].rearrange("p (a c) -> p a c", a=4)
                        tts = sorted(set([(st // 512)] if stp == 1 else ([(st + i * stp) // 512 for i in (0, 127)] if stp == 4 else [0, 1, 2, 3])))
                        if stp == 4:
                            tts = list(range(tts[0], tts[-1] + 1))
                        accres = [("acc", t) for t in tts]
                        if g == 0:
                            P.add("dve", lambda h, accv=accv, src=src: h.tensor_copy(out=accv, in_=src),
                                  reads=[("ps", bo)], writes=accres)
                        else:
                            P.add("dve", lambda h, accv=accv, src=src: h.tensor_tensor(out=accv, in0=accv, in1=src, op=ALU.add),
                                  reads=[("ps", bo)] + accres, writes=accres)
                for tt in range(4):
                    cs = slice(512 * tt, 512 * tt + 512)
                    y2 = tt % 2
                    for a in range(2):
                        P.add("dve", lambda h, a=a, cs=cs: h.reciprocal(out=acc[:64, 2 * a + 1, cs], in_=acc[:64, 2 * a + 1, cs]),
                              reads=[("acc", tt)], writes=[("acc", tt)])
                        P.add("dve", lambda h, a=a, cs=cs, y2=y2: h.tensor_tensor(
                            out=yb[:64, y2, a, :], in0=acc[:64, 2 * a, cs], in1=acc[:64, 2 * a + 1, cs], op=ALU.mult),
                            reads=[("acc", tt)], writes=[("yb", y2, a)])
                    for n_ in range(KC):
                        b = new_bank()
                        for a in range(2):
                            P.add("pe", lambda h, b=b, a=a, n_=n_, y2=y2, hp=hp: h.matmul(
                                banks[b][:, :], lhsT=wo[:64, 2 * hp + a, n_ * 128:(n_ + 1) * 128], rhs=yb[:64, y2, a, :],
                                start=(a == 0), stop=(a == 1)),
                                reads=[("w", so, 0), ("yb", y2, a)], writes=[("ps", b)])
                        P.add("dve", lambda h, b=b, n_=n_, cs=cs: h.tensor_tensor(
                            out=xT[:, n_, cs], in0=xT[:, n_, cs], in1=banks[b][:, :], op=ALU.add),
                            reads=[("ps", b)] + xT_res(n_, tt), writes=xT_res(n_, tt))
            for n_ in range(KC):
                b = new_bank()
                for hh in range(8):
                    P.add("pe", lambda h, b=b, hh=hh, n_=n_: h.matmul(
                        banks[b][:, :NS], lhsT=wo[:64, hh, n_ * 128:(n_ + 1) * 128], rhs=ysT[:64, hh, :],
                        start=(hh == 0), stop=(hh == 7)),
                        reads=[("w", so, 0), "ysT"], writes=[("ps", b)])
                P.add("dve", lambda h, b=b, n_=n_: h.tensor_tensor(
                    out=xT[:, n_, TP:TP + NS], in0=xT[:, n_, TP:TP + NS], in1=banks[b][:, :NS], op=ALU.add),
                    reads=[("ps", b)] + xT_res(n_, "s"), writes=xT_res(n_, "s"))

        def sample_attention(li, skv, ysT, caches):
            SD = 3
            ctile = alloc([SD, 1024], F32)
            prod = alloc([SD, 512], F32)
            prodn = alloc([SD, 512], F32)
            sc = alloc([SD, 8], F32)
            scn = alloc([SD, 8], F32)
            pc = alloc([SD, 8], F32)
            pnb = alloc([SD, 8], F32)
            pv = alloc([SD, 512], F32)
            us = alloc([512], F32)
            rds = alloc([8], F32)
            onesel = alloc([256], F32)
            bdm = alloc([512], F32)
            mbc = alloc([4], F32)
            mbn = alloc([48], F32)
            P.add("sp", lambda h: h.dma_start(out=onesel[:, :], in_=onesel_d[:, :]), writes=["onesel"], dkey="sc0")
            P.add("sp", lambda h: h.dma_start(out=bdm[:8, :], in_=bdm_d[:, :]), writes=["bdm"], dkey="sc1")
            P.add("sp", lambda h: h.dma_start(out=mbc[:, :], in_=mbc_d[:, :]), writes=["mbc"], dkey="sc2")
            P.add("sp", lambda h: h.dma_start(out=mbn[:NS, :], in_=mbn_d[:, :]), writes=["mbn"], dkey="sc3")
            BU, BD = 6, 7
            nbank = [0]

            def rb():
                b = nbank[0] % 6
                nbank[0] += 1
                return b
            cnt = [0]
            first = [True]
            jobs = []
            for bl in range(4):
                jobs.append((0, bl, 0, [0, 1, 2, 3]))
                for g in (1, 2):
                    for t in range(4):
                        jobs.append((g, bl, t, [t]))
            njobs_tok = sum(len(j[3]) for j in jobs)
            done = [0]
            units = []
            for (g, bl, t0, ts) in jobs:
                cs = cnt[0] % SD
                cnt[0] += 1
                for ti_, t in enumerate(ts):
                    units.append((g, bl, t0, t, DIL[g], cs, ti_ == 0))
            nun = len(units)

            def front(j):
                g, bl, t0, t, d, cs, load = units[j]
                tok = bl * 4 + t
                k = j % SD
                if load:
                    P.add("sp", lambda h: h.dma_start(
                        out=ctile[:, cs, :], in_=caches[g][li, bl, t0:t0 + 127 * d + 1:d, :]),
                        writes=[("ctile", cs)], dkey=f"ct{cs}")
                bq = rb()
                P.add("pe", lambda h: h.matmul(
                    banks[bq][:, :], lhsT=ident[:NS, tok:tok + 1].broadcast_to([NS, 128]), rhs=skv[:NS, 0, g * 512:(g + 1) * 512],
                    start=True, stop=True), reads=["ident", ("skv", 0, g)], writes=[("ps", bq)])
                P.add("dve", lambda h: h.tensor_tensor(
                    out=prod[:, k, :], in0=ctile[:, cs, 0:512], in1=banks[bq][:, :], op=ALU.mult),
                    reads=[("ps", bq), ("ctile", cs)], writes=[("prod", k)])
                P.add("dve", lambda h: h.tensor_reduce(
                    out=sc[:, k, :], in_=prod[:, k, :].rearrange("p (h d) -> p h d", d=64), axis=AX.X, op=ALU.add),
                    reads=[("prod", k)], writes=[("sc", k)])
                if g == 0:
                    P.add("act", lambda h: h.activation(out=pc[:, k, :], in_=sc[:, k, :], func=AF.Exp,
                                                        scale=0.125, bias=mbc[:, t:t + 1]),
                          reads=[("sc", k), "mbc"], writes=[("pc", k)])
                else:
                    P.add("act", lambda h: h.activation(out=pc[:, k, :], in_=sc[:, k, :], func=AF.Exp, scale=0.125),
                          reads=[("sc", k)], writes=[("pc", k)])
                P.add("dve", lambda h: h.tensor_tensor(
                    out=prodn[:NS, k, :], in0=skv[:NS, 1, g * 512:(g + 1) * 512], in1=banks[bq][:NS, :], op=ALU.mult),
                    reads=[("ps", bq), ("skv", 1, g)], writes=[("prodn", k)])
                P.add("dve", lambda h: h.tensor_reduce(
                    out=scn[:NS, k, :], in_=prodn[:NS, k, :].rearrange("p (h d) -> p h d", d=64), axis=AX.X, op=ALU.add),
                    reads=[("prodn", k)], writes=[("scn", k)])
                pn = pnb[:, k, :]
                P.add("act", lambda h: h.activation(
                    out=pn[:NS, :], in_=scn[:NS, k, :], func=AF.Exp, scale=0.125,
                    bias=mbn[:NS, g * 16 + tok:g * 16 + tok + 1]),
                    reads=[("scn", k), "mbn"], writes=[("pn", k)])

            def back(j):
                g, bl, t0, t, d, cs, load = units[j]
                tok = bl * 4 + t
                k = j % SD
                f = (j == 0)
                last = (j == nun - 1)
                pn = pnb[:, k, :]
                bp = rb()
                P.add("pe", lambda h: h.matmul(
                    banks[bp][:8, :], lhsT=pc[:, k, :], rhs=ctile[:, cs, 512:1024], start=True, stop=False),
                    reads=[("pc", k), ("ctile", cs)], writes=[("ps", bp)])
                P.add("pe", lambda h: h.matmul(
                    banks[bp][:8, :], lhsT=pn[:NS, :], rhs=skv[:NS, 2, g * 512:(g + 1) * 512], start=False, stop=True),
                    reads=[("pn", k), ("skv", 2, g)], writes=[("ps", bp)])
                P.add("dve", lambda h: h.tensor_tensor(
                    out=pv[:8, k, :], in0=banks[bp][:8, :], in1=bdm[:8, :], op=ALU.mult),
                    reads=[("ps", bp), "bdm"], writes=[("pv", k)])
                P.add("pe", lambda h: h.matmul(
                    banks[BU][:NS, :], lhsT=onesel[:8, tok * 16:(tok + 1) * 16], rhs=pv[:8, k, :],
                    start=f, stop=last), reads=[("pv", k), "onesel"], writes=[("ps", BU)])
                P.add("pe", lambda h: h.matmul(
                    banks[BD][:NS, :8], lhsT=onesel[:, tok * 16:(tok + 1) * 16], rhs=pc[:, k, :],
                    start=f, stop=False), reads=[("pc", k), "onesel"], writes=[("ps", BD)])
                P.add("pe", lambda h: h.matmul(
                    banks[BD][:NS, :8], lhsT=onesel[:NS, tok * 16:(tok + 1) * 16], rhs=pn[:NS, :],
                    start=False, stop=last), reads=[("pn", k), "onesel"], writes=[("ps", BD)])

            for j in range(nun + SD - 1):
                if j < nun:
                    front(j)
                if j - (SD - 1) >= 0:
                    back(j - (SD - 1))
            P.add("dve", lambda h: h.reciprocal(out=rds[:NS, :], in_=banks[BD][:NS, :8]), reads=[("ps", BD)], writes=["rds"])
            P.add("dve", lambda h: h.tensor_tensor(
                out=us[:NS, :].rearrange("p (h d) -> p h d", d=64), in0=banks[BU][:NS, :].rearrange("p (h d) -> p h d", d=64),
                in1=rds[:NS, :].unsqueeze(2).broadcast_to([NS, 8, 64]), op=ALU.mult),
                reads=[("ps", BU), "rds"], writes=["us"])
            for hh in range(8):
                b = rb()
                P.add("pe", lambda h, b=b, hh=hh: h.transpose(out=banks[b][:64, :NS], in_=us[:NS, hh * 64:(hh + 1) * 64],
                                                              identity=ident[:NS, :NS]),
                      reads=["us", "ident"], writes=[("ps", b)])
                P.add("act", lambda h, b=b, hh=hh: h.activation(out=ysT[:64, hh, :], in_=banks[b][:64, :NS], func=AF.Copy),
                      reads=[("ps", b)], writes=["ysT"])

        def mixer_b(L):
            li = L // 2
            wuv = w_uv_b[li].rearrange("(kc p) n -> p kc n", p=128)
            wob = w_out_b[li].rearrange("(kc p) n -> p kc n", p=128)
            phase_begin()
            hq = alloc([KC, 528], BF16)
            gT = alloc([24, 528], BF16)
            vb = alloc([4, DV], BF16)
            vs = alloc([DV], F32)
            vsb = alloc([DV], BF16)
            gbuf = alloc([2, 512], F32)
            wmb = alloc([8, 128], BF16)
            wsb = alloc([8, NS], BF16)
            bsb = alloc([8, 128], F32)
            bss = alloc([8, NS], F32)
            tmp = alloc([2, 512], F32)
            tmp2 = alloc([512], F32)
            stats = alloc([6, 6], F32)
            mv = alloc([4], F32)
            lneps = alloc([1], F32)
            wmf = tmp2[:, :].rearrange("p (a b) -> p a b", a=4)
            P.add("dve", lambda h: h.memset(lneps[:, :], LN_EPS), writes=["lneps"])
            for half in range(2):
                P.add("sp", lambda h, half=half: h.dma_start(
                    out=tmp[:, half, :].rearrange("p (g t) -> p g t", g=4),
                    in_=wspT_d[li, 4 * half:4 * half + 4].rearrange("g s t -> s g t")),
                    writes=[("tmp", half)], dkey=f"bw{half}")
                P.add("sp", lambda h, half=half: h.dma_start(out=gbuf[:, half, :128], in_=trilT_d[:, :]),
                      writes=[("gbuf", half)], dkey=f"gb{half}")
                P.add("dve", lambda h, half=half: h.tensor_tensor(
                    out=wmb[:, 4 * half:4 * half + 4, :], in0=tmp[:, half, :].rearrange("p (g t) -> p g t", g=4),
                    in1=gbuf[:, half, :128].unsqueeze(1).broadcast_to([128, 4, 128]), op=ALU.mult),
                    reads=[("tmp", half), ("gbuf", half)], writes=["wmb"])
            P.add("sp", lambda h: h.dma_start(out=tmp2[:NS, :128].rearrange("p (g t) -> p g t", g=8),
                                              in_=wsmT_d[li].rearrange("g s t -> s g t")), writes=["tmp2"], dkey="bw2")
            P.add("sp", lambda h: h.dma_start(out=tmp2[:NS, 128:144], in_=trilS_d[:, :]), writes=["tmp2b"], dkey="bw3")
            P.add("dve", lambda h: h.tensor_tensor(
                out=wsb[:NS, :, :], in0=tmp2[:NS, :128].rearrange("p (g t) -> p g t", g=8),
                in1=tmp2[:NS, 128:144].unsqueeze(1).broadcast_to([NS, 8, NS]), op=ALU.mult),
                reads=["tmp2", "tmp2b"], writes=["wsb"])
            P.add("sp", lambda h: h.dma_start(out=bsb[:, :, :], in_=bsp_d[li].partition_broadcast(128)), writes=["bsb"], dkey="bw4")
            P.add("sp", lambda h: h.dma_start(out=bss[:, :, :], in_=bsm_d[li].partition_broadcast(128)), writes=["bss"], dkey="bw5")
            P.barrier()
            sqv = vs[:, 0:2048].bitcast(BF16).rearrange("p (k c) -> p k c", k=KC)
            rstdv = vs[:, 2048:3072].rearrange("p (a c) -> p a c", a=2)
            for q in range(4):
                tiles = [TILES[q]] + ([TILES[4]] if q == 0 else [])
                chunks = [(c4, 128, c4 * 128) for c4 in range(4)] + ([("s", NS, 512)] if q == 0 else [])
                for ti, tile in enumerate(tiles):
                    n = tile[2]
                    rmsnorm(tile, DEPTH + L, lambda kc, ti=ti, n=n: hq[:, kc, 512 * ti:512 * ti + n],
                            lambda kc, ti=ti: ("hq", kc, ti), sqv, rstdv, sqres="vs", rstdres="vs")
                for sl3 in range(3):
                    s_ = new_slot()
                    wv = wring[:, s_, :].rearrange("p (k n) -> p k n", k=KC)
                    c0 = DV + sl3 * 1024
                    P.add("pool", lambda h, wv=wv, c0=c0: h.dma_start(out=wv[:, :, :], in_=wuv[:, :, c0:c0 + 1024]),
                          writes=[("w", s_, 0), ("w", s_, 1)], dkey=f"w{s_}_0")
                    for (cn, m, l0) in chunks:
                        ti = 1 if cn == "s" else 0
                        for cb2 in range(2):
                            b = new_bank()
                            for kc in range(KC):
                                P.add("pe", lambda h, b=b, kc=kc, m=m, l0=l0, wv=wv, cb2=cb2: h.matmul(
                                    banks[b][:m, :], lhsT=hq[:, kc, l0:l0 + m], rhs=wv[:, kc, cb2 * 512:(cb2 + 1) * 512],
                                    start=(kc == 0), stop=(kc == KC - 1)),
                                    reads=[("w", s_, 0), ("hq", kc, ti)], writes=[("ps", b)])
                            cb = sl3 * 2 + cb2
                            dst = vs[:NS, cb * 512:(cb + 1) * 512] if cn == "s" else vb[:, cn, cb * 512:(cb + 1) * 512]
                            P.add("act", lambda h, b=b, m=m, dst=dst: h.activation(out=dst, in_=banks[b][:m, :], func=AF.Gelu_apprx_tanh),
                                  reads=[("ps", b)], writes=[("v", cn, cb), ("vn", cn, cb)] + (["vs"] if cn == "s" else []))
                for (cn, m, l0) in chunks:
                    src = (lambda cb: vs[:NS, cb * 512:(cb + 1) * 512]) if cn == "s" else (lambda cb, cn=cn: vb[:, cn, cb * 512:(cb + 1) * 512])
                    for cb in range(6):
                        P.add("dve", lambda h, cb=cb, src=src, m=m: h.bn_stats(out=stats[:m, cb, :], in_=src(cb)),
                              reads=[("v", cn, cb)] + (["vs"] if cn == "s" else []), writes=[("stats", cb)])
                    P.add("dve", lambda h, m=m: h.bn_aggr(out=mv[:m, 0:2], in_=stats[:m, :, :].rearrange("p a b -> p (a b)")),
                          reads=[("stats", cb) for cb in range(6)], writes=["mv"])
                    P.add("act", lambda h, m=m: h.activation(out=mv[:m, 2:3], in_=mv[:m, 1:2], func=AF.Sqrt, bias=lneps[:m, :], scale=1.0),
                          reads=["mv", "lneps"], writes=["mv2"])
                    P.add("dve", lambda h, m=m: h.reciprocal(out=mv[:m, 3:4], in_=mv[:m, 2:3]), reads=["mv2"], writes=["mv3"])
                    for cb in range(6):
                        gk = cb % 2
                        P.add("sp", lambda h, cb=cb, gk=gk: h.dma_start(
                            out=gbuf[:, gk, :], in_=ln_g_d[li:li + 1, cb * 512:(cb + 1) * 512].partition_broadcast(128)[:, 0, :]),
                            writes=[("gbuf", gk)], dkey=f"gb{gk}")
                        t_ = tmp[:m, gk, :]
                        P.add("dve", lambda h, cb=cb, src=src, m=m, t_=t_: h.tensor_scalar(
                            out=t_, in0=src(cb), scalar1=mv[:m, 0:1], scalar2=mv[:m, 3:4], op0=ALU.subtract, op1=ALU.mult),
                            reads=[("v", cn, cb), "mv", "mv3"], writes=[("tmp", gk)])
                        P.add("dve", lambda h, m=m, t_=t_, gk=gk: h.tensor_tensor(out=t_, in0=t_, in1=gbuf[:m, gk, :], op=ALU.mult),
                              reads=[("tmp", gk), ("gbuf", gk)], writes=[("tmp", gk)])
                        P.add("sp", lambda h, cb=cb, gk=gk: h.dma_start(
                            out=gbuf[:, gk, :], in_=ln_b_d[li:li + 1, cb * 512:(cb + 1) * 512].partition_broadcast(128)[:, 0, :]),
                            writes=[("gbuf", gk)], dkey=f"gb{gk}")
                        if cn == "s":
                            P.add("dve", lambda h, cb=cb, m=m, t_=t_, gk=gk: h.tensor_tensor(
                                out=vs[:NS, cb * 512:(cb + 1) * 512], in0=t_, in1=gbuf[:m, gk, :], op=ALU.add),
                                reads=[("tmp", gk), ("gbuf", gk)], writes=[("v", cn, cb)])
                            P.add("act", lambda h, cb=cb: h.activation(out=vsb[:NS, cb * 512:(cb + 1) * 512],
                                                                   in_=vs[:NS, cb * 512:(cb + 1) * 512], func=AF.Copy),
                                  reads=[("v", cn, cb), "vs"], writes=[("vn", cn, cb)])
                        else:
                            P.add("dve", lambda h, cb=cb, m=m, t_=t_, gk=gk, cn=cn: h.tensor_tensor(
                                out=vb[:, cn, cb * 512:(cb + 1) * 512], in0=t_, in1=gbuf[:m, gk, :], op=ALU.add),
                                reads=[("tmp", gk), ("gbuf", gk), ("v", cn, cb)], writes=[("vn", cn, cb)])
                    if cn == "s":
                        P.add("sp", lambda h: h.dma_start(out=vchunk[li, :, :], in_=vs[:NS, :]),
                              reads=[("v", "s", cb) for cb in range(6)] + ["vs"], writes=["vchunk"], dkey="vch")
                for sl3 in range(3):
                    s_ = new_slot()
                    wu = wring[:, s_, :].rearrange("p (k n) -> p k n", k=KC)
                    c0 = sl3 * 1024
                    P.add("pool", lambda h, wu=wu, c0=c0: h.dma_start(out=wu[:, :, :], in_=wuv[:, :, c0:c0 + 1024]),
                          writes=[("w", s_, 0), ("w", s_, 1)], dkey=f"w{s_}_0")
                    for jj in range(8):
                        j = sl3 * 8 + jj
                        gg = j // 3
                        cb = j // 4
                        jo = (j % 4) * 128
                        for ti, tile in enumerate(tiles):
                            n = tile[2]
                            l0 = 512 * ti
                            bu_ = new_bank()
                            for kc in range(KC):
                                P.add("pe", lambda h, bu_=bu_, kc=kc, jj=jj, wu=wu, l0=l0, n=n: h.matmul(
                                    banks[bu_][:, :n], lhsT=wu[:, kc, jj * 128:(jj + 1) * 128], rhs=hq[:, kc, l0:l0 + n],
                                    start=(kc == 0), stop=(kc == KC - 1)),
                                    reads=[("w", s_, 0), ("hq", kc, ti)], writes=[("ps", bu_)])
                            tk = ti
                            P.add("act", lambda h, bu_=bu_, n=n, tk=tk: h.activation(out=tmp[:, tk, :n], in_=banks[bu_][:, :n], func=AF.Gelu_apprx_tanh),
                                  reads=[("ps", bu_)], writes=[("tmp", tk)])
                            bm = new_bank()
                            if ti == 0:
                                for c4 in range(4):
                                    P.add("pe", lambda h, bm=bm, c4=c4, cb=cb, jo=jo, gg=gg: h.matmul(
                                        banks[bm][:, c4 * 128:(c4 + 1) * 128], lhsT=vb[:, c4, cb * 512 + jo:cb * 512 + jo + 128],
                                        rhs=wmb[:, gg, :], start=True, stop=True),
                                        reads=[("vn", c4, cb), "wmb"], writes=[("ps", bm)])
                                P.add("dve", lambda h, bm=bm, gg=gg: h.tensor_tensor(
                                    out=tmp2[:, :].rearrange("p (a c) -> p a c", a=4),
                                    in0=banks[bm][:, :].rearrange("p (a c) -> p a c", a=4),
                                    in1=bsb[:, gg, :].unsqueeze(1).broadcast_to([128, 4, 128]), op=ALU.add),
                                    reads=[("ps", bm), "bsb"], writes=["tmp2"])
                            else:
                                P.add("pe", lambda h, bm=bm, cb=cb, jo=jo, gg=gg: h.matmul(
                                    banks[bm][:, :NS], lhsT=vsb[:NS, cb * 512 + jo:cb * 512 + jo + 128],
                                    rhs=wsb[:NS, gg, :], start=True, stop=True),
                                    reads=[("vn", "s", cb), "wsb"], writes=[("ps", bm)])
                                P.add("dve", lambda h, bm=bm, gg=gg: h.tensor_tensor(
                                    out=tmp2[:, :NS], in0=banks[bm][:, :NS], in1=bss[:, gg, :], op=ALU.add),
                                    reads=[("ps", bm), "bss"], writes=["tmp2"])
                            P.add("dve", lambda h, j=j, l0=l0, n=n, tk=tk: h.tensor_tensor(
                                out=gT[:, j, l0:l0 + n], in0=tmp[:, tk, :n], in1=tmp2[:, :n], op=ALU.mult),
                                reads=[("tmp", tk), "tmp2"], writes=[("gT", j, ti)])
                for n0 in range(0, KC, 2):
                    s_ = new_slot()
                    wo_ = wring[:, s_, :24 * 256].rearrange("p (k n) -> p k n", k=24)
                    P.add("pool", lambda h, wo_=wo_, n0=n0: h.dma_start(out=wo_[:, :, :], in_=wob[:, :, n0 * 128:(n0 + 2) * 128]),
                          writes=[("w", s_, 0), ("w", s_, 1)], dkey=f"w{s_}_0")
                    for ti, tile in enumerate(tiles):
                        tname, x0, n = tile
                        l0 = 512 * ti
                        for nn in range(2):
                            b = new_bank()
                            for kc in range(24):
                                P.add("pe", lambda h, b=b, kc=kc, nn=nn, wo_=wo_, l0=l0, n=n: h.matmul(
                                    banks[b][:, :n], lhsT=wo_[:, kc, nn * 128:(nn + 1) * 128], rhs=gT[:, kc, l0:l0 + n],
                                    start=(kc == 0), stop=(kc == 23)),
                                    reads=[("w", s_, 0), ("gT", kc, ti)], writes=[("ps", b)])
                            kco = n0 + nn
                            P.add("dve", lambda h, b=b, kco=kco, x0=x0, n=n: h.tensor_tensor(
                                out=xT[:, kco, x0:x0 + n], in0=xT[:, kco, x0:x0 + n], in1=banks[b][:, :n], op=ALU.add),
                                reads=[("ps", b)] + xT_res(kco, tname), writes=xT_res(kco, tname))

        state["xs_out"] = 0

        phase_begin()
        xs = alloc([2, D], F32)
        load_x(xs)
        need_layout = True
        for L in range(depth):
            for which in (1, 2):
                for hf in range(2):
                    if need_layout:
                        phase_begin()
                        xn = alloc([KC, 1040], BF16)
                        act = alloc([NJ, 1040], BF16)
                        sq = alloc([KC, 512], BF16)
                        rstd = alloc([2, 512], F32)
                        tmp = alloc([3, 512], F32)
                        need_layout = False
                    ffn(L, which, hf, xn, act, sq, rstd, tmp)
                if which == 1 and cfg.get("mixers", True):
                    if L % 2 == 0:
                        mixer_a(L)
                    else:
                        mixer_b(L)
                    need_layout = True
        phase_begin()
        xs = alloc([2, D], F32)
        yT = alloc([KC, 512], F32)
        sq = alloc([KC, 512], BF16)
        rstd = alloc([2, 512], F32)
        final_out(xs, yT, sq, rstd)

        for op in P.ops:
            if op.dkey is not None and op.dkey not in dsems:
                dsems[op.dkey] = es.enter_context(nc.semaphore("d_" + op.dkey))
        run = P.emit(nc, None, sems, dsems)
        last_out = {}
        for op in P.ops:
            if op.dkey is not None and op.eng == "sp":
                last_out[op.dkey] = max(op.val, last_out.get(op.dkey, 0))
        with nc.Block() as block:
            @block.tensor
            def _(h):
                run("pe", h)

            @block.scalar
            def _(h):
                run("act", h)

            @block.vector
            def _(h):
                run("dve", h)

            @block.gpsimd
            def _(h):
                run("pool", h)

            @block.sync
            def _(h):
                run("sp", h)
                for k, v in last_out.items():
                    h.wait_ge(dsems[k], v)
    return nc


def make_inputs(cfg, inputs):
    f = lambda k: np.asarray(inputs[k], dtype=np.float32)
    xp, xs_ = f("x_prompt"), f("x_sample")
    kq = np.arange(128)
    prev = (kq[:, None] >= kq[None, :]).astype(np.float32)
    own = (kq[:, None] <= kq[None, :]).astype(np.float32)
    maskN = np.concatenate([prev, own, prev, own], axis=1)
    onesel = np.zeros((128, 256), np.float32)
    for tok in range(16):
        onesel[:, tok * 16 + tok] = 1.0
    bdm = np.zeros((8, 512), np.float32)
    for h in range(8):
        bdm[h, h * 64:(h + 1) * 64] = 1.0
    NEG = -30000.0
    mbc = np.where(kq[:, None] >= np.arange(4)[None, :], 0.0, NEG).astype(np.float32)
    mbn = np.full((16, 48), NEG, np.float32)
    for tok in range(16):
        b, t = tok // 4, tok % 4
        for tok2 in range(16):
            b2, t2 = tok2 // 4, tok2 % 4
            if b2 == b and t2 <= t:
                mbn[tok2, 0 * 16 + tok] = 0.0
        mbn[tok, 16 + tok] = 0.0
        mbn[tok, 32 + tok] = 0.0
    wsp = f("w_spatial")
    wspT = np.ascontiguousarray(wsp.transpose(0, 1, 3, 2))
    wsmT = np.zeros((2, 8, 16, 16), np.float32)
    trilS = np.zeros((16, 16), np.float32)
    for b in range(4):
        wsmT[:, :, 4 * b:4 * b + 4, 4 * b:4 * b + 4] = wspT[:, :, :4, :4]
        for s_ in range(4):
            for t_ in range(s_, 4):
                trilS[4 * b + s_, 4 * b + t_] = 1.0
    bsp = f("b_spatial")
    bsm = np.ascontiguousarray(np.tile(bsp[:, :, :4], (1, 1, 4)))
    trilT = own.copy()
    shared = {
        "ident": np.eye(128, dtype=np.float32),
        "norm_ffn1": f("norm_ffn1"), "norm_ffn2": f("norm_ffn2"), "norm_mix": f("norm_mix"),
        "norm_final": f("norm_final").reshape(1, D),
        "w_ffn1_in": f("w_ffn1_in"), "w_ffn1_out": f("w_ffn1_out"),
        "w_ffn2_in": f("w_ffn2_in"), "w_ffn2_out": f("w_ffn2_out"),
        "w_qkv_a": f("w_qkv_a"), "w_out_a": f("w_out_a"),
        "maskN": maskN, "onesel": onesel, "bdm": bdm, "mbc": mbc, "mbn": mbn,
        "w_uv_b": f("w_uv_b"), "w_out_b": f("w_out_b"), "ln_v_gain": f("ln_v_gain"), "ln_v_bias": f("ln_v_bias"),
        "wspT": wspT, "wsmT": wsmT, "b_spatial": bsp, "bsm": bsm, "trilT": trilT, "trilS": trilS,
    }
    c128, c512, c2048 = f("cache_kv_w128"), f("cache_kv_w512"), f("cache_kv_w2048")
    maps = []
    for c in range(8):
        p, hf = c // 2, c % 2
        m = dict(shared)
        m["xin"] = np.ascontiguousarray(
            np.concatenate([xp[p, hf * TP:(hf + 1) * TP], xs_[4 * c:4 * c + 4].reshape(NS, D)], axis=0))
        m["maskH"] = maskN if hf == 1 else np.concatenate([np.zeros_like(prev), own, np.zeros_like(prev), own], axis=1)
        m["cache128"] = np.ascontiguousarray(c128[:, 4 * c:4 * c + 4].reshape(2, 4, 128, 1024))
        m["cache512"] = np.ascontiguousarray(c512[:, 4 * c:4 * c + 4].reshape(2, 4, 512, 1024))
        m["cache2048"] = np.ascontiguousarray(c2048[:, 4 * c:4 * c + 4].reshape(2, 4, 2048, 1024))
        maps.append(m)
    return maps


_NC_CACHE = {}


def run_cfg(cfg, inputs, trace=False):
    key = tuple(sorted(cfg.items()))
    if key not in _NC_CACHE:
        _NC_CACHE[key] = build_nc(cfg)
    nc = _NC_CACHE[key]
    maps = make_inputs(cfg, inputs)
    names = set()
    for alloc in nc.allocations:
        if isinstance(alloc, mybir.MemoryLocationSet) and alloc.kind == "ExternalInput":
            names.add(alloc.memorylocations[0].name)
    maps = [{k: v for k, v in m.items() if k in names} for m in maps]
    return run_bass_kernel_spmd(nc, maps, core_ids=list(range(8)), trace=trace)


def assemble(res):
    yp = np.zeros((4, 4096, D), np.float32)
    ys = np.zeros((32, 4, D), np.float32)
    kvp = [np.zeros((2, 4, w, 2, 8, 64), np.float32) for w in (128, 512, 2048)]
    kvs = [np.zeros((2, 32, 4, 2, 8, 64), np.float32) for _ in range(3)]
    vch = np.zeros((2, 32, 4, DV), np.float32)
    for c in range(8):
        p, hf = c // 2, c % 2
        r = res.results[c]
        y = r["yout"]
        yp[p, hf * TP:(hf + 1) * TP] = y[:TP]
        ys[4 * c:4 * c + 4] = y[TP:].reshape(4, 4, D)
        if hf == 1:
            for g, nm in enumerate(("kvp128", "kvp512", "kvp2048")):
                kvp[g][:, p] = r[nm].reshape(2, -1, 2, 8, 64)
        for g in range(3):
            kvs[g][:, 4 * c:4 * c + 4] = r["kvs"][:, g].reshape(2, 4, 4, 2, 8, 64)
        vch[:, 4 * c:4 * c + 4] = r["vchunk"].reshape(2, 4, 4, DV)
    return (yp, ys, kvp[0], kvp[1], kvp[2], kvs[0], kvs[1], kvs[2], vch)


def kernel(**inputs):
    res = run_cfg({}, inputs)
    return assemble(res)
```

```python
import numpy as np
import concourse.bass as bass
import concourse.mybir as mybir
from concourse.bass_utils import run_bass_kernel_spmd

F32 = mybir.dt.float32
BF16 = mybir.dt.bfloat16
AF = mybir.ActivationFunctionType
ALU = mybir.AluOpType
AX = mybir.AxisListType

D = 1024
KC = 8
TP = 2048
NS = 16
TT = TP + NS
DFF = 2816
NJ = 22
DEPTH = 4
QKVW = 4608
DV = 3072
SLOT = 8192
NSLOT = 3
RMS_EPS = 1e-6
LN_EPS = 1e-5

ENGS = ("pe", "act", "dve", "pool", "sp")


class Op:
    __slots__ = ("eng", "fn", "deps", "signal", "val", "dkey", "dinc", "idx")

    def __init__(self, eng, fn, dkey=None):
        self.eng = eng
        self.fn = fn
        self.deps = []
        self.signal = False
        self.val = None
        self.dkey = dkey
        self.dinc = 16


class Prog:
    def __init__(self):
        self.ops = []
        self.last_w = {}
        self.readers = {}
        self.last_eng = {}
        self.last_dma = {}
        self.bar_deps = []
        self.bar_pending = set()

    def barrier(self):
        self.bar_deps = list(self.last_eng.values()) + list(self.last_dma.values())
        self.bar_pending = set(ENGS)

    def add(self, eng, fn, reads=(), writes=(), dkey=None, dinc=16):
        op = Op(eng, fn, dkey)
        op.dinc = dinc
        deps = []
        ring_only = bool(writes) and all(isinstance(w, tuple) and w[0] == "w" for w in writes) and not reads
        if eng in self.bar_pending and not ring_only:
            self.bar_pending.discard(eng)
            deps.extend(self.bar_deps)
        for r in list(reads) + list(writes):
            lw = self.last_w.get(r)
            if lw is not None:
                deps.append(lw)
        for w in writes:
            deps.extend(self.readers.get(w, {}).values())
        seen = set()
        for d in deps:
            if id(d) in seen or d is op:
                continue
            seen.add(id(d))
            if d.eng == "pe" and eng == "pe" and d.dkey is None and dkey is None:
                continue
            op.deps.append(d)
            d.signal = True
        rkey = dkey if dkey is not None else eng
        for r in reads:
            self.readers.setdefault(r, {})[rkey] = op
        for w in writes:
            self.last_w[w] = op
            self.readers[w] = {}
        op.idx = len(self.ops)
        self.ops.append(op)
        if dkey is not None:
            self.last_dma[dkey] = op
        else:
            self.last_eng[eng] = op
        return op

    def emit(self, nc, block, sems, dsems):
        cnt = {e: 0 for e in ENGS}
        dcnt = {}
        for op in self.ops:
            if op.dkey is not None:
                dcnt[op.dkey] = dcnt.get(op.dkey, 0) + op.dinc
                op.val = dcnt[op.dkey]
            elif op.signal:
                cnt[op.eng] += 1
                op.val = cnt[op.eng]
        for op in self.ops:
            if op.dkey == "const":
                op.val = dcnt["const"]
        per_eng = {e: [o for o in self.ops if o.eng == e] for e in ENGS}

        def run(engname, h):
            waited = {}
            for op in per_eng[engname]:
                need = {}
                for d in op.deps:
                    key = ("d", d.dkey) if d.dkey is not None else ("e", d.eng)
                    if d.val > need.get(key, 0):
                        need[key] = d.val
                for key, v in need.items():
                    if waited.get(key, 0) >= v:
                        continue
                    sem = dsems[key[1]] if key[0] == "d" else sems[key[1]]
                    h.wait_ge(sem, v)
                    waited[key] = v
                ins = op.fn(h)
                if op.dkey is not None:
                    if op.dinc == 1:
                        ins.then_inc(dsems[op.dkey])
                    else:
                        ins.then_inc(dsems[op.dkey], op.dinc)
                elif op.signal:
                    ins.then_inc(sems[op.eng], 1)
            for op in per_eng[engname]:
                pass

        return run


def build_nc(cfg):
    depth = cfg.get("depth", DEPTH)
    nc = bass.Bass("TRN2", target_bir_lowering=False)
    P = Prog()

    def din(name, shape, dt=F32):
        return nc.dram_tensor(name, list(shape), dt, kind="ExternalInput").ap()

    def dout(name, shape, dt=F32):
        return nc.dram_tensor(name, list(shape), dt, kind="ExternalOutput").ap()

    xin = din("xin", [TT, D])
    ident_d = din("ident", [128, 128])
    norm_ffn1 = din("norm_ffn1", [DEPTH, D])
    norm_ffn2 = din("norm_ffn2", [DEPTH, D])
    norm_mix = din("norm_mix", [DEPTH, D])
    norm_final = din("norm_final", [1, D])
    w_ffn1_in = din("w_ffn1_in", [DEPTH, D, 2 * DFF])
    w_ffn1_out = din("w_ffn1_out", [DEPTH, DFF, D])
    w_ffn2_in = din("w_ffn2_in", [DEPTH, D, 2 * DFF])
    w_ffn2_out = din("w_ffn2_out", [DEPTH, DFF, D])
    yout = dout("yout", [TT, D])
    w_qkv_a = din("w_qkv_a", [2, D, QKVW])
    w_out_a = din("w_out_a", [2, 512, D])
    cache128 = din("cache128", [2, 4, 128, 1024])
    cache512 = din("cache512", [2, 4, 512, 1024])
    cache2048 = din("cache2048", [2, 4, 2048, 1024])
    maskN_d = din("maskN", [128, 512])
    maskH_d = din("maskH", [128, 512])
    onesel_d = din("onesel", [128, 256])
    bdm_d = din("bdm", [8, 512])
    mbc_d = din("mbc", [128, 4])
    mbn_d = din("mbn", [NS, 48])
    w_uv_b = din("w_uv_b", [2, D, 2 * DV])
    w_out_b = din("w_out_b", [2, DV, D])
    ln_g_d = din("ln_v_gain", [2, DV])
    ln_b_d = din("ln_v_bias", [2, DV])
    wspT_d = din("wspT", [2, 8, 128, 128])
    wsmT_d = din("wsmT", [2, 8, NS, NS])
    bsp_d = din("b_spatial", [2, 8, 128])
    bsm_d = din("bsm", [2, 8, NS])
    trilT_d = din("trilT", [128, 128])
    trilS_d = din("trilS", [NS, NS])
    kvp128 = dout("kvp128", [2, 128, 2, 512])
    kvp512 = dout("kvp512", [2, 512, 2, 512])
    kvp2048 = dout("kvp2048", [2, 2048, 2, 512])
    kvs = dout("kvs", [2, 3, NS, 2, 512])
    vchunk = dout("vchunk", [2, NS, DV])
    kvx_ins = [nc.dram_tensor(f"kvx_in{i}", [3072, 2048], BF16).ap() for i in range(2)]
    kvx_outs = [[nc.dram_tensor(f"kvx_out{i}_{c}", [512, 2048], BF16).ap() for c in range(12)] for i in range(2)]

    import contextlib
    es = contextlib.ExitStack()
    with es:
        def sb(name, shape, dt):
            return es.enter_context(nc.sbuf_tensor(name, list(shape), dt))

        xT = sb("xT", [128, KC, TT], F32)
        wring = sb("wring", [128, NSLOT, SLOT], BF16)
        ident = sb("ident_sb", [128, 128], F32)
        ones_b = sb("ones_b", [128, 128], BF16)
        gam = sb("gam", [128, 3 * DEPTH + 1, KC], F32)
        epsb = sb("epsb", [128, 1], F32)
        ARENA_WORDS = 24000
        arena = sb("arena", [128, ARENA_WORDS], F32)
        ar = {"off": 0}

        def phase_begin():
            P.barrier()
            ar["off"] = 0

        def alloc(shape, dt):
            nel = int(np.prod(shape))
            words = (nel * (2 if dt == BF16 else 4) + 3) // 4
            o = ar["off"]
            assert o + words <= ARENA_WORDS, (o, words)
            ar["off"] = o + words
            v = arena[:, o:o + words]
            if dt == BF16:
                v = v.bitcast(BF16)[:, :nel]
            if len(shape) == 2:
                v = v.rearrange("p (a b) -> p a b", a=shape[0])
            elif len(shape) == 3:
                v = v.rearrange("p (a b c) -> p a b c", a=shape[0], b=shape[1])
            return v
        banks = [es.enter_context(nc.psum_tensor(f"ps{b}", [128, 512], F32)) for b in range(8)]

        sems = {e: es.enter_context(nc.semaphore(f"s_{e}")) for e in ENGS}
        dsems = {}

        state = {"bank": 0, "slot": 0, "tmp": 0, "rstd": 0}

        def new_bank():
            b = state["bank"]
            state["bank"] = (b + 1) % 8
            return b

        def new_slot():
            s = state["slot"]
            state["slot"] = (s + 1) % NSLOT
            return s

        P.add("sp", lambda h: h.dma_start(out=ident[:], in_=ident_d[:, :]), writes=["ident"], dkey="const")
        for i, nd in enumerate((norm_ffn1, norm_mix, norm_ffn2)):
            for L in range(DEPTH):
                P.add("sp", lambda h, nd=nd, L=L, i=i: h.dma_start(
                    out=gam[:, i * DEPTH + L, :], in_=nd[L, :].rearrange("(kc p) -> p kc", p=128),
                    allow_slow_non_contiguous=True), writes=[("gam", i * DEPTH + L)], dkey="const")
        P.add("sp", lambda h: h.dma_start(
            out=gam[:, 3 * DEPTH, :], in_=norm_final[0, :].rearrange("(kc p) -> p kc", p=128),
            allow_slow_non_contiguous=True), writes=[("gam", 3 * DEPTH)], dkey="const")
        P.add("dve", lambda h: h.memset(ones_b[:], 1.0), writes=["ones"])

        TILES = [(t, 512 * t, 512) for t in range(4)] + [("s", TP, NS)]
        HALF_TILES = [[TILES[0], TILES[1], TILES[4]], [TILES[2], TILES[3]]]

        def lcol(ti):
            return 512 * ti

        def load_x(xs):
            blocks = [(128 * i, 128) for i in range(16)] + [(TP, NS)]
            for bi, (r0, n) in enumerate(blocks):
                sl = bi % 2
                tname = ("s" if r0 >= TP else r0 // 512)
                P.add("sp", lambda h, r0=r0, n=n, sl=sl: h.dma_start(out=xs[:n, sl, :], in_=xin[r0:r0 + n, :]),
                      writes=[("xs", sl)], dkey=f"xs{sl}")
                for q in range(2):
                    b = new_bank()
                    for k4 in range(4):
                        kc = 4 * q + k4
                        P.add("pe", lambda h, b=b, k4=k4, kc=kc, n=n, sl=sl: h.transpose(
                            out=banks[b][:, k4 * 128:k4 * 128 + n], in_=xs[:n, sl, kc * 128:(kc + 1) * 128],
                            identity=ident[:n, :n]),
                            reads=[("xs", sl), "ident"], writes=[("ps", b)])
                    P.add("act" if q == 0 else "dve",
                          (lambda h, b=b, q=q, r0=r0, n=n: h.activation(
                              out=xT[:, 4 * q:4 * q + 4, r0:r0 + n],
                              in_=banks[b][:, :].rearrange("p (k c) -> p k c", k=4)[:, :, :n], func=AF.Copy))
                          if q == 0 else
                          (lambda h, b=b, q=q, r0=r0, n=n: h.tensor_copy(
                              out=xT[:, 4 * q:4 * q + 4, r0:r0 + n],
                              in_=banks[b][:, :].rearrange("p (k c) -> p k c", k=4)[:, :, :n])),
                          reads=[("ps", b)], writes=[("xT", kc, tname, (r0 % 512) // 128) for kc in range(4 * q, 4 * q + 4)])

        def xT_res(kc, tname):
            if tname == "s":
                return [("xT", kc, "s", 0)]
            return [("xT", kc, tname, i) for i in range(4)]

        def rmsnorm(tile, gidx, out_fn, out_res, sq, rstd, sqres="sq", rstdres=None):
            tname, x0, n = tile
            rs = state["rstd"]
            state["rstd"] = (rs + 1) % 2
            rres = ("rstd", rs) if rstdres is None else rstdres
            P.add("act", lambda h: h.activation(out=sq[:, :, :n], in_=xT[:, :, x0:x0 + n], func=AF.Square),
                  reads=[r for kc in range(KC) for r in xT_res(kc, tname)], writes=[sqres])
            b = new_bank()
            for kc in range(KC):
                P.add("pe", lambda h, kc=kc: h.matmul(banks[b][:, :n], lhsT=ones_b[:], rhs=sq[:, kc, :n],
                                                      start=(kc == 0), stop=(kc == KC - 1)),
                      reads=[sqres, "ones"], writes=[("ps", b)])
            P.add("act", lambda h: h.activation(out=rstd[:, rs, :n], in_=banks[b][:, :n], func=AF.Sqrt,
                                                scale=1.0 / D, bias=epsb[:]),
                  reads=[("ps", b), "epsb"], writes=[rres])
            P.add("dve", lambda h: h.reciprocal(out=rstd[:, rs, :n], in_=rstd[:, rs, :n]),
                  reads=[rres], writes=[rres])
            for kc in range(KC):
                P.add("dve", lambda h, kc=kc: h.scalar_tensor_tensor(
                    out=out_fn(kc), in0=xT[:, kc, x0:x0 + n], scalar=gam[:, gidx, kc:kc + 1],
                    in1=rstd[:, rs, :n], op0=ALU.mult, op1=ALU.mult),
                    reads=xT_res(kc, tname) + [rres, ("gam", gidx)], writes=[out_res(kc)])

        P.add("dve", lambda h: h.memset(epsb[:], RMS_EPS), writes=["epsb"])

        def ffn(L, which, hf, xn, act, sq, rstd, tmp):
            tiles = HALF_TILES[hf]
            w_in = (w_ffn1_in if which == 1 else w_ffn2_in)[L].rearrange("(kc p) n -> p kc n", p=128)
            w_out = (w_ffn1_out if which == 1 else w_ffn2_out)[L].rearrange("(kc p) n -> p kc n", p=128)
            gidx = (0 if which == 1 else 2) * DEPTH + L
            for ti, tile in enumerate(tiles):
                n = tile[2]
                rmsnorm(tile, gidx, lambda kc, ti=ti, n=n: xn[:, kc, lcol(ti):lcol(ti) + n],
                        lambda kc, ti=ti: ("xn", kc, ti), sq, rstd)
            for j0 in range(0, NJ, 4):
                jb = min(4, NJ - j0)
                s = new_slot()
                wv = wring[:, s, :2 * KC * jb * 128].rearrange("p (g k n) -> p g k n", g=2, k=KC)
                for g in range(2):
                    c0 = g * DFF + j0 * 128
                    P.add("pool", lambda h, g=g, c0=c0, wv=wv, jb=jb: h.dma_start(
                        out=wv[:, g, :, :], in_=w_in[:, :, c0:c0 + jb * 128]),
                        writes=[("w", s, g)], dkey=f"w{s}_{g}")
                for ti, tile in enumerate(tiles):
                    n = tile[2]
                    l0 = lcol(ti)
                    for jj in range(jb):
                        bg = new_bank()
                        bu = new_bank()
                        for g, b in ((0, bg), (1, bu)):
                            for kc in range(KC):
                                P.add("pe", lambda h, g=g, b=b, kc=kc, jj=jj, wv=wv, l0=l0, n=n: h.matmul(
                                    banks[b][:, :n], lhsT=wv[:, g, kc, jj * 128:(jj + 1) * 128],
                                    rhs=xn[:, kc, l0:l0 + n], start=(kc == 0), stop=(kc == KC - 1)),
                                    reads=[("w", s, g), ("xn", kc, ti)], writes=[("ps", b)])
                        tb = state["tmp"]
                        state["tmp"] = (tb + 1) % 3
                        P.add("act", lambda h, bg=bg, tb=tb, n=n: h.activation(
                            out=tmp[:, tb, :n], in_=banks[bg][:, :n], func=AF.Silu),
                            reads=[("ps", bg)], writes=[("tmp", tb)])
                        P.add("dve", lambda h, bu=bu, tb=tb, n=n, j=j0 + jj, l0=l0: h.tensor_tensor(
                            out=act[:, j, l0:l0 + n], in0=tmp[:, tb, :n], in1=banks[bu][:, :n], op=ALU.mult),
                            reads=[("tmp", tb), ("ps", bu)], writes=[("act", j0 + jj, ti)])
            for n0 in range(0, KC, 2):
                s = new_slot()
                wv = wring[:, s, :NJ * 256].rearrange("p (k n) -> p k n", k=NJ)
                P.add("pool", lambda h, wv=wv, n0=n0: h.dma_start(
                    out=wv[:, :, :], in_=w_out[:, :, n0 * 128:(n0 + 2) * 128]),
                    writes=[("w", s, 0), ("w", s, 1)], dkey=f"w{s}_0")
                for ti, tile in enumerate(tiles):
                    tname, x0, n = tile
                    l0 = lcol(ti)
                    for nn in range(2):
                        b = new_bank()
                        for kc in range(NJ):
                            P.add("pe", lambda h, b=b, kc=kc, nn=nn, wv=wv, l0=l0, n=n: h.matmul(
                                banks[b][:, :n], lhsT=wv[:, kc, nn * 128:(nn + 1) * 128],
                                rhs=act[:, kc, l0:l0 + n], start=(kc == 0), stop=(kc == NJ - 1)),
                                reads=[("w", s, 0), ("w", s, 1), ("act", kc, ti)], writes=[("ps", b)])
                        kco = n0 + nn
                        P.add("dve", lambda h, b=b, kco=kco, x0=x0, n=n: h.scalar_tensor_tensor(
                            out=xT[:, kco, x0:x0 + n], in0=banks[b][:, :n], scalar=0.5,
                            in1=xT[:, kco, x0:x0 + n], op0=ALU.mult, op1=ALU.add),
                            reads=[("ps", b)] + xT_res(kco, tname), writes=xT_res(kco, tname))

        def final_out(xs, yT, sq, rstd):
            gidx = 3 * DEPTH
            for tile in TILES:
                tname, x0, n = tile
                rmsnorm(tile, gidx, lambda kc, n=n: yT[:, kc, :n], lambda kc: ("yT", kc), sq, rstd)
                for sub in range(0, n, 128):
                    m = min(128, n - sub)
                    sl = state["xs_out"]
                    state["xs_out"] = (sl + 1) % 2
                    for q in range(2):
                        b = new_bank()
                        for k4 in range(4):
                            kc = 4 * q + k4
                            P.add("pe", lambda h, b=b, k4=k4, kc=kc, sub=sub, m=m: h.transpose(
                                out=banks[b][:m, k4 * 128:(k4 + 1) * 128], in_=yT[:, kc, sub:sub + m],
                                identity=ident[:, :]),
                                reads=[("yT", kc), "ident"], writes=[("ps", b)])
                        P.add("act" if q == 0 else "dve",
                              (lambda h, b=b, q=q, sl=sl, m=m: h.activation(
                                  out=xs[:m, sl, 512 * q:512 * q + 512], in_=banks[b][:m, :], func=AF.Copy))
                              if q == 0 else
                              (lambda h, b=b, q=q, sl=sl, m=m: h.tensor_copy(
                                  out=xs[:m, sl, 512 * q:512 * q + 512], in_=banks[b][:m, :])),
                              reads=[("ps", b)], writes=[("xs", sl, q)])
                    r0 = x0 + sub
                    P.add("sp", lambda h, sl=sl, m=m, r0=r0: h.dma_start(out=yout[r0:r0 + m, :], in_=xs[:m, sl, :]),
                          reads=[("xs", sl, 0), ("xs", sl, 1)], writes=[("yo", sl)], dkey=f"yo{sl}")


        DIL = (1, 4, 16)

        def blk_cols(g, blk):
            d = DIL[g]
            npr = 16 // d
            r, nb = blk // npr, blk % npr
            st = r + 128 * nb * d
            return st, d

        def out_blocks(g):
            d = DIL[g]
            npr = 16 // d
            return [r * npr + npr - 1 for r in range(d)]

        def mixer_a(L):
            li = L // 2
            wqkv = w_qkv_a[li].rearrange("(kc p) n -> p kc n", p=128)
            kvx_in, kvx_out = kvx_ins[li], kvx_outs[li]
            kvp = (kvp128, kvp512, kvp2048)
            caches = (cache128, cache512, cache2048)
            phase_begin()
            hT = alloc([KC, TT], BF16)
            mark0 = ar["off"]
            sq = alloc([KC, 512], BF16)
            rstd = alloc([2, 512], F32)
            for tile in TILES:
                tname, x0, n = tile
                rmsnorm(tile, DEPTH + L, lambda kc, x0=x0, n=n: hT[:, kc, x0:x0 + n],
                        lambda kc, tname=tname: ("hT", kc, tname), sq, rstd)
            P.barrier()
            ar["off"] = mark0
            ysT = alloc([8, NS], BF16)
            maskN = alloc([512], BF16)
            maskH = alloc([512], BF16)
            P.add("pool", lambda h: h.dma_start(out=maskN[:, :], in_=maskN_d[:, :]), writes=["masks"], dkey="mk")
            P.add("pool", lambda h: h.dma_start(out=maskH[:, :], in_=maskH_d[:, :]), writes=["masks"], dkey="mk")
            identb = alloc([128], BF16)
            P.add("pool", lambda h: h.dma_start(out=identb[:, :], in_=ident_d[:, :]), writes=["masks"], dkey="mk")
            markY = ar["off"]
            skv = alloc([3, 1536], F32)
            mark1 = ar["off"]
            kst = alloc([2, 2048], BF16)
            vst = alloc([16, 512], BF16)
            fst = alloc([2, 512], F32)
            fi = [0]
            for g in range(3):
                d = DIL[g]
                npr = 16 // d
                sA = new_slot()
                wkv = wring[:, sA, :].rearrange("p (a k n) -> p a k n", a=2, k=KC)
                for a in range(2):
                    c0 = ((a + 1) * 3 + g) * 512
                    P.add("pool", lambda h, a=a, c0=c0, wkv=wkv: h.dma_start(out=wkv[:, a, :, :], in_=wqkv[:, :, c0:c0 + 512]),
                          writes=[("w", sA, a)], dkey=f"w{sA}_{a}")
                sB = new_slot()
                wqv = wring[:, sB, :KC * 512].rearrange("p (k n) -> p k n", k=KC)
                P.add("pool", lambda h, g=g, wqv=wqv: h.dma_start(out=wqv[:, :, :], in_=wqkv[:, :, g * 512:(g + 1) * 512]),
                      writes=[("w", sB, 0), ("w", sB, 1)], dkey=f"w{sB}_0")
                for hp in range(4):
                    ks = hp % 2
                    u = g * 4 + hp
                    for tt in range(4):
                        b = new_bank()
                        for kc in range(KC):
                            P.add("pe", lambda h, b=b, kc=kc, hp=hp, tt=tt, wkv=wkv: h.matmul(
                                banks[b][:, :], lhsT=wkv[:, 0, kc, hp * 128:(hp + 1) * 128],
                                rhs=hT[:, kc, 512 * tt:512 * tt + 512], start=(kc == 0), stop=(kc == KC - 1)),
                                reads=[("w", sA, 0), ("hT", kc, tt)], writes=[("ps", b)])
                        w_ = 512 // d
                        P.add("act", lambda h, b=b, ks=ks, tt=tt, d=d, w_=w_: h.activation(
                            out=kst[:, ks, :].rearrange("p (r m) -> p r m", r=d)[:, :, w_ * tt:w_ * (tt + 1)],
                            in_=banks[b][:, :].rearrange("p (m r) -> p r m", r=d), func=AF.Copy),
                            reads=[("ps", b)], writes=[("kst", ks, tt)])
                    P.add("sp", lambda h, ks=ks, u=u: h.dma_start(out=kvx_in[u * 128:(u + 1) * 128, :], in_=kst[:, ks, :]),
                          reads=[("kst", ks, tt) for tt in range(4)], writes=[("kvx_in", "k", u)], dkey=f"kst{ks}")
                oblks = out_blocks(g)
                for blk in range(16):
                    st, stp = blk_cols(g, blk)
                    b = new_bank()
                    for kc in range(KC):
                        P.add("pe", lambda h, b=b, kc=kc, st=st, stp=stp, wkv=wkv: h.matmul(
                            banks[b][:, :], lhsT=hT[:, kc, st:st + 127 * stp + 1:stp], rhs=wkv[:, 1, kc, :],
                            start=(kc == 0), stop=(kc == KC - 1)),
                            reads=[("w", sA, 1)] + [("hT", kc, t) for t in range(4)], writes=[("ps", b)])
                    P.add("act", lambda h, b=b, blk=blk: h.activation(out=vst[:, blk, :], in_=banks[b][:, :], func=AF.Copy),
                          reads=[("ps", b)], writes=[("vst", blk)])
                    if blk in oblks:
                        row0 = oblks.index(blk)
                        f = fi[0] % 2
                        fi[0] += 1
                        P.add("dve", lambda h, b=b, f=f: h.tensor_copy(out=fst[:, f, :], in_=banks[b][:, :]),
                              reads=[("ps", b), ("vst", blk)], writes=[("fst", f)])
                        P.add("sp", lambda h, f=f, g=g, row0=row0, d=d: h.dma_start(
                            out=kvp[g][li, row0:row0 + 127 * d + 1:d, 1, :], in_=fst[:, f, :]),
                            reads=[("fst", f)], writes=[("kvp", g, blk, 1)], dkey=f"fst{f}")
                        b2 = new_bank()
                        for kc in range(KC):
                            P.add("pe", lambda h, b2=b2, kc=kc, st=st, stp=stp, wkv=wkv: h.matmul(
                                banks[b2][:, :], lhsT=hT[:, kc, st:st + 127 * stp + 1:stp], rhs=wkv[:, 0, kc, :],
                                start=(kc == 0), stop=(kc == KC - 1)),
                                reads=[("w", sA, 0)] + [("hT", kc, t) for t in range(4)], writes=[("ps", b2)])
                        f = fi[0] % 2
                        fi[0] += 1
                        P.add("dve", lambda h, b2=b2, f=f: h.tensor_copy(out=fst[:, f, :], in_=banks[b2][:, :]),
                              reads=[("ps", b2)], writes=[("fst", f)])
                        P.add("sp", lambda h, f=f, g=g, row0=row0, d=d: h.dma_start(
                            out=kvp[g][li, row0:row0 + 127 * d + 1:d, 0, :], in_=fst[:, f, :]),
                            reads=[("fst", f)], writes=[("kvp", g, blk, 0)], dkey=f"fst{f}")
                vreg = kvx_in[1536 + g * 512:1536 + (g + 1) * 512, :].rearrange("a (b f) -> (a b) f", f=512)
                P.add("sp", lambda h, vreg=vreg: h.dma_start(
                    out=vreg.rearrange("(blk p) f -> p blk f", p=128), in_=vst[:, :, :]),
                    reads=[("vst", blk) for blk in range(16)], writes=[("kvx_in", "v", g)], dkey="vst")
                for idx, (wv_, a_, res) in enumerate(((wqv, None, ("w", sB, 0)), (wkv, 0, ("w", sA, 0)), (wkv, 1, ("w", sA, 1)))):
                    b = new_bank()
                    for kc in range(KC):
                        P.add("pe", lambda h, b=b, kc=kc, wv_=wv_, a_=a_: h.matmul(
                            banks[b][:NS, :], lhsT=hT[:, kc, TP:TP + NS],
                            rhs=(wv_[:, kc, :] if a_ is None else wv_[:, a_, kc, :]),
                            start=(kc == 0), stop=(kc == KC - 1)),
                            reads=[res, ("hT", kc, "s")], writes=[("ps", b)])
                    P.add("dve", lambda h, b=b, idx=idx, g=g: h.tensor_copy(
                        out=skv[:NS, idx, g * 512:(g + 1) * 512], in_=banks[b][:NS, :]),
                        reads=[("ps", b)], writes=[("skv", idx, g)])
                for idx in (1, 2):
                    P.add("sp", lambda h, idx=idx, g=g: h.dma_start(
                        out=kvs[li, g, :, idx - 1, :], in_=skv[:NS, idx, g * 512:(g + 1) * 512]),
                        reads=[("skv", idx, g)], writes=[("kvs", g, idx)], dkey="kvs")
            if cfg.get("a_stop", 9) <= 1:
                return
            for c in range(12):
                rd = [("kvx_in", "k", 2 * c), ("kvx_in", "k", 2 * c + 1)] if c < 6 else [("kvx_in", "v", (c - 6) // 2)]
                P.add("pool", lambda h, c=c: h.collective_compute(
                    "AllGather", ALU.bypass, replica_groups=[[0, 1], [2, 3], [4, 5], [6, 7]],
                    ins=[kvx_in[256 * c:256 * (c + 1), :].opt()], outs=[kvx_out[c].opt()]),
                    reads=rd, writes=[("kvx_out", c)], dkey=f"cc{li}_{c}", dinc=1)
            P.barrier()
            ar["off"] = mark1
            if cfg.get("a_stop", 9) <= 2:
                return
            sample_attention(li, skv, ysT, caches)
            P.barrier()
            ar["off"] = markY
            if cfg.get("a_stop", 9) <= 3:
                return
            qT = alloc([2048], BF16)
            kTo = alloc([2048], BF16)
            kTh = alloc([2048], BF16)
            vo = alloc([16, 128], BF16)
            vh = alloc([16, 128], BF16)
            acc = alloc([4, 2048], F32)
            pT = alloc([2, 512], BF16)
            yb = alloc([2, 2, 512], BF16)
            so = new_slot()
            wo = wring[:, so, :].rearrange("p (h n) -> p h n", h=8)
            P.add("pool", lambda h: h.dma_start(out=wo[:64, :, :], in_=w_out_a[li].rearrange("(h d) n -> d h n", d=64)),
                  writes=[("w", so, 0), ("w", so, 1)], dkey=f"w{so}_0")
            pi = [0]
            for hp in range(4):
                for g in range(3):
                    d = DIL[g]
                    npr = 16 // d
                    u = g * 4 + hp
                    sq_ = new_slot()
                    while sq_ == so:
                        sq_ = new_slot()
                    wq1 = wring[:, sq_, :KC * 128].rearrange("p (k n) -> p k n", k=KC)
                    P.add("pool", lambda h, wq1=wq1, g=g, hp=hp: h.dma_start(
                        out=wq1[:, :, :], in_=wqkv[:, :, g * 512 + hp * 128:g * 512 + (hp + 1) * 128]),
                        writes=[("w", sq_, 0), ("w", sq_, 1)], dkey=f"w{sq_}_0")
                    P.add("sp", lambda h, u=u: h.dma_start(out=kTo[:, :], in_=kvx_in[u * 128:(u + 1) * 128, :]),
                          reads=[("kvx_in", "k", u)], writes=["kTo"], dkey="kTo")
                    P.add("sp", lambda h, u=u, d=d, npr=npr: h.dma_start(
                        out=kTh[:, :d * 128].rearrange("p (r c) -> p r c", r=d),
                        in_=kvx_out[u // 2][(u % 2) * 128:(u % 2) * 128 + 128, :].rearrange(
                            "p (r nb c) -> p r nb c", r=d, nb=npr)[:, :, npr - 1, :]),
                        reads=[("kvx_out", u // 2)], writes=["kTh"], dkey="kTh")
                    vreg_i = kvx_in[1536 + g * 512:1536 + (g + 1) * 512, :].rearrange("a (b f) -> (a b) f", f=512)
                    P.add("sp", lambda h, vreg_i=vreg_i, hp=hp: h.dma_start(
                        out=vo[:, :, :], in_=vreg_i.rearrange("(blk p) f -> p blk f", p=128)[:, :, hp * 128:(hp + 1) * 128]),
                        reads=[("kvx_in", "v", g)], writes=["vo"], dkey="vo")
                    for c2 in range(2):
                        cch = 6 + 2 * g + c2
                        vview = kvx_out[cch][0:256, :].rearrange("a (b f) -> (a b) f", f=512).rearrange(
                            "(blk p) f -> p blk f", p=128)
                        if npr <= 8:
                            cntb = 8 // npr
                            srcv = vview[:, npr - 1:8:npr, hp * 128:(hp + 1) * 128]
                            dstv = vh[:, c2 * cntb:(c2 + 1) * cntb, :]
                        else:
                            if c2 == 0:
                                continue
                            srcv = vview[:, 7:8, hp * 128:(hp + 1) * 128]
                            dstv = vh[:, 0:1, :]
                        P.add("sp", lambda h, srcv=srcv, dstv=dstv: h.dma_start(out=dstv, in_=srcv),
                              reads=[("kvx_out", cch)], writes=["vh"], dkey=f"vh{c2}")
                    for tt in range(4):
                        b = new_bank()
                        for kc in range(KC):
                            P.add("pe", lambda h, b=b, kc=kc, tt=tt, wq1=wq1: h.matmul(
                                banks[b][:, :], lhsT=wq1[:, kc, :], rhs=hT[:, kc, 512 * tt:512 * tt + 512],
                                start=(kc == 0), stop=(kc == KC - 1)),
                                reads=[("w", sq_, 0), ("hT", kc, tt)], writes=[("ps", b)])
                        w_ = 512 // d
                        P.add("act", lambda h, b=b, tt=tt, d=d, w_=w_: h.activation(
                            out=qT[:, :].rearrange("p (r m) -> p r m", r=d)[:, :, w_ * tt:w_ * (tt + 1)],
                            in_=banks[b][:, :].rearrange("p (m r) -> p r m", r=d), func=AF.Copy, scale=0.125),
                            reads=[("ps", b)], writes=[("qT", tt)])
                    for blk in range(16):
                        r, nb = blk // npr, blk % npr
                        c0 = blk * 128
                        pp = pi[0] % 2
                        pi[0] += 1
                        mk = maskH if nb == 0 else maskN
                        for a in range(2):
                            bs = new_bank()
                            pa = slice(64 * a, 64 * a + 64)
                            if nb == 0:
                                kprev = kTh[pa, r * 128:(r + 1) * 128]
                                kres = "kTh"
                            else:
                                kprev = kTo[pa, c0 - 128:c0]
                                kres = "kTo"
                            P.add("pe", lambda h, bs=bs, mk=mk: h.matmul(
                                banks[bs][:, 0:256], lhsT=identb[:, :], rhs=mk[:, 0:256], start=True, stop=False),
                                reads=["masks"], writes=[("ps", bs)])
                            P.add("pe", lambda h, bs=bs, pa=pa, kprev=kprev, c0=c0: h.matmul(
                                banks[bs][:, 0:128], lhsT=kprev, rhs=qT[pa, c0:c0 + 128], start=False, stop=False),
                                reads=[kres] + [("qT", t) for t in range(4)], writes=[("ps", bs)])
                            P.add("pe", lambda h, bs=bs, pa=pa, c0=c0: h.matmul(
                                banks[bs][:, 128:256], lhsT=kTo[pa, c0:c0 + 128], rhs=qT[pa, c0:c0 + 128],
                                start=False, stop=True),
                                reads=["kTo"] + [("qT", t) for t in range(4)], writes=[("ps", bs)])
                            P.add("act", lambda h, bs=bs, pp=pp, a=a: h.activation(
                                out=pT[:, pp, 256 * a:256 * a + 256], in_=banks[bs][:, 0:256], func=AF.Exp),
                                reads=[("ps", bs)], writes=[("pT", pp, a)])
                        bo = new_bank()
                        for a in range(2):
                            vprev = vh[:, r, 64 * a:64 * a + 64] if nb == 0 else vo[:, blk - 1, 64 * a:64 * a + 64]
                            vres = "vh" if nb == 0 else "vo"
                            for c, lh, rs_ in ((0, vprev, vres), (1, vo[:, blk, 64 * a:64 * a + 64], "vo")):
                                P.add("pe", lambda h, bo=bo, a=a, c=c, lh=lh, pp=pp: h.matmul(
                                    banks[bo][:64, (2 * a) * 128:(2 * a + 1) * 128], lhsT=lh,
                                    rhs=pT[:, pp, (2 * a + c) * 128:(2 * a + c + 1) * 128], start=(c == 0), stop=(c == 1)),
                                    reads=[rs_, ("pT", pp, a)], writes=[("ps", bo)])
                            for c in range(2):
                                P.add("pe", lambda h, bo=bo, a=a, c=c, pp=pp: h.matmul(
                                    banks[bo][:64, (2 * a + 1) * 128:(2 * a + 2) * 128], lhsT=ones_b[:, :64],
                                    rhs=pT[:, pp, (2 * a + c) * 128:(2 * a + c + 1) * 128], start=(c == 0), stop=(c == 1)),
                                    reads=["ones", ("pT", pp, a)], writes=[("ps", bo)])
                        st, stp = blk_cols(g, blk)
                        accv = acc[:64, :, st:st + 127 * stp + 1:stp]
                        src = banks[bo][:64, :].rearrange("p (a c) -> p a c", a=4)
                        tts = sorted(set([(st // 512)] if stp == 1 else ([(st + i * stp) // 512 for i in (0, 127)] if stp == 4 else [0, 1, 2, 3])))
                        if stp == 4:
                            tts = list(range(tts[0], tts[-1] + 1))
                        accres = [("acc", t) for t in tts]
                        if g == 0:
                            P.add("dve", lambda h, accv=accv, src=src: h.tensor_copy(out=accv, in_=src),
                                  reads=[("ps", bo)], writes=accres)
                        else:
                            P.add("dve", lambda h, accv=accv, src=src: h.tensor_tensor(out=accv, in0=accv, in1=src, op=ALU.add),
                                  reads=[("ps", bo)] + accres, writes=accres)
                for tt in range(4):
                    cs = slice(512 * tt, 512 * tt + 512)
                    y2 = tt % 2
                    for a in range(2):
                        P.add("dve", lambda h, a=a, cs=cs: h.reciprocal(out=acc[:64, 2 * a + 1, cs], in_=acc[:64, 2 * a + 1, cs]),
                              reads=[("acc", tt)], writes=[("acc", tt)])
                        P.add("dve", lambda h, a=a, cs=cs, y2=y2: h.tensor_tensor(
                            out=yb[:64, y2, a, :], in0=acc[:64, 2 * a, cs], in1=acc[:64, 2 * a + 1, cs], op=ALU.mult),
                            reads=[("acc", tt)], writes=[("yb", y2, a)])
                    for n_ in range(KC):
                        b = new_bank()
                        for a in range(2):
                            P.add("pe", lambda h, b=b, a=a, n_=n_, y2=y2, hp=hp: h.matmul(
                                banks[b][:, :], lhsT=wo[:64, 2 * hp + a, n_ * 128:(n_ + 1) * 128], rhs=yb[:64, y2, a, :],
                                start=(a == 0), stop=(a == 1)),
                                reads=[("w", so, 0), ("yb", y2, a)], writes=[("ps", b)])
                        P.add("dve", lambda h, b=b, n_=n_, cs=cs: h.tensor_tensor(
                            out=xT[:, n_, cs], in0=xT[:, n_, cs], in1=banks[b][:, :], op=ALU.add),
                            reads=[("ps", b)] + xT_res(n_, tt), writes=xT_res(n_, tt))
            for n_ in range(KC):
                b = new_bank()
                for hh in range(8):
                    P.add("pe", lambda h, b=b, hh=hh, n_=n_: h.matmul(
                        banks[b][:, :NS], lhsT=wo[:64, hh, n_ * 128:(n_ + 1) * 128], rhs=ysT[:64, hh, :],
                        start=(hh == 0), stop=(hh == 7)),
                        reads=[("w", so, 0), "ysT"], writes=[("ps", b)])
                P.add("dve", lambda h, b=b, n_=n_: h.tensor_tensor(
                    out=xT[:, n_, TP:TP + NS], in0=xT[:, n_, TP:TP + NS], in1=banks[b][:, :NS], op=ALU.add),
                    reads=[("ps", b)] + xT_res(n_, "s"), writes=xT_res(n_, "s"))

        def sample_attention(li, skv, ysT, caches):
            SD = 3
            ctile = alloc([SD, 1024], F32)
            prod = alloc([SD, 512], F32)
            prodn = alloc([SD, 512], F32)
            sc = alloc([SD, 8], F32)
            scn = alloc([SD, 8], F32)
            pc = alloc([SD, 8], F32)
            pnb = alloc([SD, 8], F32)
            pv = alloc([SD, 512], F32)
            us = alloc([512], F32)
            rds = alloc([8], F32)
            onesel = alloc([256], F32)
            bdm = alloc([512], F32)
            mbc = alloc([4], F32)
            mbn = alloc([48], F32)
            P.add("sp", lambda h: h.dma_start(out=onesel[:, :], in_=onesel_d[:, :]), writes=["onesel"], dkey="sc0")
            P.add("sp", lambda h: h.dma_start(out=bdm[:8, :], in_=bdm_d[:, :]), writes=["bdm"], dkey="sc1")
            P.add("sp", lambda h: h.dma_start(out=mbc[:, :], in_=mbc_d[:, :]), writes=["mbc"], dkey="sc2")
            P.add("sp", lambda h: h.dma_start(out=mbn[:NS, :], in_=mbn_d[:, :]), writes=["mbn"], dkey="sc3")
            BU, BD = 6, 7
            nbank = [0]

            def rb():
                b = nbank[0] % 6
                nbank[0] += 1
                return b
            cnt = [0]
            first = [True]
            jobs = []
            for bl in range(4):
                jobs.append((0, bl, 0, [0, 1, 2, 3]))
                for g in (1, 2):
                    for t in range(4):
                        jobs.append((g, bl, t, [t]))
            njobs_tok = sum(len(j[3]) for j in jobs)
            done = [0]
            units = []
            for (g, bl, t0, ts) in jobs:
                cs = cnt[0] % SD
                cnt[0] += 1
                for ti_, t in enumerate(ts):
                    units.append((g, bl, t0, t, DIL[g], cs, ti_ == 0))
            nun = len(units)

            def front(j):
                g, bl, t0, t, d, cs, load = units[j]
                tok = bl * 4 + t
                k = j % SD
                if load:
                    P.add("sp", lambda h: h.dma_start(
                        out=ctile[:, cs, :], in_=caches[g][li, bl, t0:t0 + 127 * d + 1:d, :]),
                        writes=[("ctile", cs)], dkey=f"ct{cs}")
                bq = rb()
                P.add("pe", lambda h: h.matmul(
                    banks[bq][:, :], lhsT=ident[:NS, tok:tok + 1].broadcast_to([NS, 128]), rhs=skv[:NS, 0, g * 512:(g + 1) * 512],
                    start=True, stop=True), reads=["ident", ("skv", 0, g)], writes=[("ps", bq)])
                P.add("dve", lambda h: h.tensor_tensor(
                    out=prod[:, k, :], in0=ctile[:, cs, 0:512], in1=banks[bq][:, :], op=ALU.mult),
                    reads=[("ps", bq), ("ctile", cs)], writes=[("prod", k)])
                P.add("dve", lambda h: h.tensor_reduce(
                    out=sc[:, k, :], in_=prod[:, k, :].rearrange("p (h d) -> p h d", d=64), axis=AX.X, op=ALU.add),
                    reads=[("prod", k)], writes=[("sc", k)])
                if g == 0:
                    P.add("act", lambda h: h.activation(out=pc[:, k, :], in_=sc[:, k, :], func=AF.Exp,
                                                        scale=0.125, bias=mbc[:, t:t + 1]),
                          reads=[("sc", k), "mbc"], writes=[("pc", k)])
                else:
                    P.add("act", lambda h: h.activation(out=pc[:, k, :], in_=sc[:, k, :], func=AF.Exp, scale=0.125),
                          reads=[("sc", k)], writes=[("pc", k)])
                P.add("dve", lambda h: h.tensor_tensor(
                    out=prodn[:NS, k, :], in0=skv[:NS, 1, g * 512:(g + 1) * 512], in1=banks[bq][:NS, :], op=ALU.mult),
                    reads=[("ps", bq), ("skv", 1, g)], writes=[("prodn", k)])
                P.add("dve", lambda h: h.tensor_reduce(
                    out=scn[:NS, k, :], in_=prodn[:NS, k, :].rearrange("p (h d) -> p h d", d=64), axis=AX.X, op=ALU.add),
                    reads=[("prodn", k)], writes=[("scn", k)])
                pn = pnb[:, k, :]
                P.add("act", lambda h: h.activation(
                    out=pn[:NS, :], in_=scn[:NS, k, :], func=AF.Exp, scale=0.125,
                    bias=mbn[:NS, g * 16 + tok:g * 16 + tok + 1]),
                    reads=[("scn", k), "mbn"], writes=[("pn", k)])

            def back(j):
                g, bl, t0, t, d, cs, load = units[j]
                tok = bl * 4 + t
                k = j % SD
                f = (j == 0)
                last = (j == nun - 1)
                pn = pnb[:, k, :]
                bp = rb()
                P.add("pe", lambda h: h.matmul(
                    banks[bp][:8, :], lhsT=pc[:, k, :], rhs=ctile[:, cs, 512:1024], start=True, stop=False),
                    reads=[("pc", k), ("ctile", cs)], writes=[("ps", bp)])
                P.add("pe", lambda h: h.matmul(
                    banks[bp][:8, :], lhsT=pn[:NS, :], rhs=skv[:NS, 2, g * 512:(g + 1) * 512], start=False, stop=True),
                    reads=[("pn", k), ("skv", 2, g)], writes=[("ps", bp)])
                P.add("dve", lambda h: h.tensor_tensor(
                    out=pv[:8, k, :], in0=banks[bp][:8, :], in1=bdm[:8, :], op=ALU.mult),
                    reads=[("ps", bp), "bdm"], writes=[("pv", k)])
                P.add("pe", lambda h: h.matmul(
                    banks[BU][:NS, :], lhsT=onesel[:8, tok * 16:(tok + 1) * 16], rhs=pv[:8, k, :],
                    start=f, stop=last), reads=[("pv", k), "onesel"], writes=[("ps", BU)])
                P.add("pe", lambda h: h.matmul(
                    banks[BD][:NS, :8], lhsT=onesel[:, tok * 16:(tok + 1) * 16], rhs=pc[:, k, :],
                    start=f, stop=False), reads=[("pc", k), "onesel"], writes=[("ps", BD)])
                P.add("pe", lambda h: h.matmul(
                    banks[BD][:NS, :8], lhsT=onesel[:NS, tok * 16:(tok + 1) * 16], rhs=pn[:NS, :],
                    start=False, stop=last), reads=[("pn", k), "onesel"], writes=[("ps", BD)])

            for j in range(nun + SD - 1):
                if j < nun:
                    front(j)
                if j - (SD - 1) >= 0:
                    back(j - (SD - 1))
            P.add("dve", lambda h: h.reciprocal(out=rds[:NS, :], in_=banks[BD][:NS, :8]), reads=[("ps", BD)], writes=["rds"])
            P.add("dve", lambda h: h.tensor_tensor(
                out=us[:NS, :].rearrange("p (h d) -> p h d", d=64), in0=banks[BU][:NS, :].rearrange("p (h d) -> p h d", d=64),
                in1=rds[:NS, :].unsqueeze(2).broadcast_to([NS, 8, 64]), op=ALU.mult),
                reads=[("ps", BU), "rds"], writes=["us"])
            for hh in range(8):
                b = rb()
                P.add("pe", lambda h, b=b, hh=hh: h.transpose(out=banks[b][:64, :NS], in_=us[:NS, hh * 64:(hh + 1) * 64],
                                                              identity=ident[:NS, :NS]),
                      reads=["us", "ident"], writes=[("ps", b)])
                P.add("act", lambda h, b=b, hh=hh: h.activation(out=ysT[:64, hh, :], in_=banks[b][:64, :NS], func=AF.Copy),
                      reads=[("ps", b)], writes=["ysT"])

        def mixer_b(L):
            li = L // 2
            wuv = w_uv_b[li].rearrange("(kc p) n -> p kc n", p=128)
            wob = w_out_b[li].rearrange("(kc p) n -> p kc n", p=128)
            phase_begin()
            hq = alloc([KC, 528], BF16)
            gT = alloc([24, 528], BF16)
            vb = alloc([4, DV], BF16)
            vs = alloc([DV], F32)
            vsb = alloc([DV], BF16)
            gbuf = alloc([2, 512], F32)
            wmb = alloc([8, 128], BF16)
            wsb = alloc([8, NS], BF16)
            bsb = alloc([8, 128], F32)
            bss = alloc([8, NS], F32)
            tmp = alloc([2, 512], F32)
            tmp2 = alloc([512], F32)
            stats = alloc([6, 6], F32)
            mv = alloc([4], F32)
            lneps = alloc([1], F32)
            wmf = tmp2[:, :].rearrange("p (a b) -> p a b", a=4)
            P.add("dve", lambda h: h.memset(lneps[:, :], LN_EPS), writes=["lneps"])
            for half in range(2):
                P.add("sp", lambda h, half=half: h.dma_start(
                    out=tmp[:, half, :].rearrange("p (g t) -> p g t", g=4),
                    in_=wspT_d[li, 4 * half:4 * half + 4].rearrange("g s t -> s g t")),
                    writes=[("tmp", half)], dkey=f"bw{half}")
                P.add("sp", lambda h, half=half: h.dma_start(out=gbuf[:, half, :128], in_=trilT_d[:, :]),
                      writes=[("gbuf", half)], dkey=f"gb{half}")
                P.add("dve", lambda h, half=half: h.tensor_tensor(
                    out=wmb[:, 4 * half:4 * half + 4, :], in0=tmp[:, half, :].rearrange("p (g t) -> p g t", g=4),
                    in1=gbuf[:, half, :128].unsqueeze(1).broadcast_to([128, 4, 128]), op=ALU.mult),
                    reads=[("tmp", half), ("gbuf", half)], writes=["wmb"])
            P.add("sp", lambda h: h.dma_start(out=tmp2[:NS, :128].rearrange("p (g t) -> p g t", g=8),
                                              in_=wsmT_d[li].rearrange("g s t -> s g t")), writes=["tmp2"], dkey="bw2")
            P.add("sp", lambda h: h.dma_start(out=tmp2[:NS, 128:144], in_=trilS_d[:, :]), writes=["tmp2b"], dkey="bw3")
            P.add("dve", lambda h: h.tensor_tensor(
                out=wsb[:NS, :, :], in0=tmp2[:NS, :128].rearrange("p (g t) -> p g t", g=8),
                in1=tmp2[:NS, 128:144].unsqueeze(1).broadcast_to([NS, 8, NS]), op=ALU.mult),
                reads=["tmp2", "tmp2b"], writes=["wsb"])
            P.add("sp", lambda h: h.dma_start(out=bsb[:, :, :], in_=bsp_d[li].partition_broadcast(128)), writes=["bsb"], dkey="bw4")
            P.add("sp", lambda h: h.dma_start(out=bss[:, :, :], in_=bsm_d[li].partition_broadcast(128)), writes=["bss"], dkey="bw5")
            P.barrier()
            sqv = vs[:, 0:2048].bitcast(BF16).rearrange("p (k c) -> p k c", k=KC)
            rstdv = vs[:, 2048:3072].rearrange("p (a c) -> p a c", a=2)
            for q in range(4):
                tiles = [TILES[q]] + ([TILES[4]] if q == 0 else [])
                chunks = [(c4, 128, c4 * 128) for c4 in range(4)] + ([("s", NS, 512)] if q == 0 else [])
                for ti, tile in enumerate(tiles):
                    n = tile[2]
                    rmsnorm(tile, DEPTH + L, lambda kc, ti=ti, n=n: hq[:, kc, 512 * ti:512 * ti + n],
                            lambda kc, ti=ti: ("hq", kc, ti), sqv, rstdv, sqres="vs", rstdres="vs")
                for sl3 in range(3):
                    s_ = new_slot()
                    wv = wring[:, s_, :].rearrange("p (k n) -> p k n", k=KC)
                    c0 = DV + sl3 * 1024
                    P.add("pool", lambda h, wv=wv, c0=c0: h.dma_start(out=wv[:, :, :], in_=wuv[:, :, c0:c0 + 1024]),
                          writes=[("w", s_, 0), ("w", s_, 1)], dkey=f"w{s_}_0")
                    for (cn, m, l0) in chunks:
                        ti = 1 if cn == "s" else 0
                        for cb2 in range(2):
                            b = new_bank()
                            for kc in range(KC):
                                P.add("pe", lambda h, b=b, kc=kc, m=m, l0=l0, wv=wv, cb2=cb2: h.matmul(
                                    banks[b][:m, :], lhsT=hq[:, kc, l0:l0 + m], rhs=wv[:, kc, cb2 * 512:(cb2 + 1) * 512],
                                    start=(kc == 0), stop=(kc == KC - 1)),
                                    reads=[("w", s_, 0), ("hq", kc, ti)], writes=[("ps", b)])
                            cb = sl3 * 2 + cb2
                            dst = vs[:NS, cb * 512:(cb + 1) * 512] if cn == "s" else vb[:, cn, cb * 512:(cb + 1) * 512]
                            P.add("act", lambda h, b=b, m=m, dst=dst: h.activation(out=dst, in_=banks[b][:m, :], func=AF.Gelu_apprx_tanh),
                                  reads=[("ps", b)], writes=[("v", cn, cb), ("vn", cn, cb)] + (["vs"] if cn == "s" else []))
                for (cn, m, l0) in chunks:
                    src = (lambda cb: vs[:NS, cb * 512:(cb + 1) * 512]) if cn == "s" else (lambda cb, cn=cn: vb[:, cn, cb * 512:(cb + 1) * 512])
                    for cb in range(6):
                        P.add("dve", lambda h, cb=cb, src=src, m=m: h.bn_stats(out=stats[:m, cb, :], in_=src(cb)),
                              reads=[("v", cn, cb)] + (["vs"] if cn == "s" else []), writes=[("stats", cb)])
                    P.add("dve", lambda h, m=m: h.bn_aggr(out=mv[:m, 0:2], in_=stats[:m, :, :].rearrange("p a b -> p (a b)")),
                          reads=[("stats", cb) for cb in range(6)], writes=["mv"])
                    P.add("act", lambda h, m=m: h.activation(out=mv[:m, 2:3], in_=mv[:m, 1:2], func=AF.Sqrt, bias=lneps[:m, :], scale=1.0),
                          reads=["mv", "lneps"], writes=["mv2"])
                    P.add("dve", lambda h, m=m: h.reciprocal(out=mv[:m, 3:4], in_=mv[:m, 2:3]), reads=["mv2"], writes=["mv3"])
                    for cb in range(6):
                        gk = cb % 2
                        P.add("sp", lambda h, cb=cb, gk=gk: h.dma_start(
                            out=gbuf[:, gk, :], in_=ln_g_d[li:li + 1, cb * 512:(cb + 1) * 512].partition_broadcast(128)[:, 0, :]),
                            writes=[("gbuf", gk)], dkey=f"gb{gk}")
                        t_ = tmp[:m, gk, :]
                        P.add("dve", lambda h, cb=cb, src=src, m=m, t_=t_: h.tensor_scalar(
                            out=t_, in0=src(cb), scalar1=mv[:m, 0:1], scalar2=mv[:m, 3:4], op0=ALU.subtract, op1=ALU.mult),
                            reads=[("v", cn, cb), "mv", "mv3"], writes=[("tmp", gk)])
                        P.add("dve", lambda h, m=m, t_=t_, gk=gk: h.tensor_tensor(out=t_, in0=t_, in1=gbuf[:m, gk, :], op=ALU.mult),
                              reads=[("tmp", gk), ("gbuf", gk)], writes=[("tmp", gk)])
                        P.add("sp", lambda h, cb=cb, gk=gk: h.dma_start(
                            out=gbuf[:, gk, :], in_=ln_b_d[li:li + 1, cb * 512:(cb + 1) * 512].partition_broadcast(128)[:, 0, :]),
                            writes=[("gbuf", gk)], dkey=f"gb{gk}")
                        if cn == "s":
                            P.add("dve", lambda h, cb=cb, m=m, t_=t_, gk=gk: h.tensor_tensor(
                                out=vs[:NS, cb * 512:(cb + 1) * 512], in0=t_, in1=gbuf[:m, gk, :], op=ALU.add),
                                reads=[("tmp", gk), ("gbuf", gk)], writes=[("v", cn, cb)])
                            P.add("act", lambda h, cb=cb: h.activation(out=vsb[:NS, cb * 512:(cb + 1) * 512],
                                                                   in_=vs[:NS, cb * 512:(cb + 1) * 512], func=AF.Copy),
                                  reads=[("v", cn, cb), "vs"], writes=[("vn", cn, cb)])
                        else:
                            P.add("dve", lambda h, cb=cb, m=m, t_=t_, gk=gk, cn=cn: h.tensor_tensor(
                                out=vb[:, cn, cb * 512:(cb + 1) * 512], in0=t_, in1=gbuf[:m, gk, :], op=ALU.add),
                                reads=[("tmp", gk), ("gbuf", gk), ("v", cn, cb)], writes=[("vn", cn, cb)])
                    if cn == "s":
                        P.add("sp", lambda h: h.dma_start(out=vchunk[li, :, :], in_=vs[:NS, :]),
                              reads=[("v", "s", cb) for cb in range(6)] + ["vs"], writes=["vchunk"], dkey="vch")
                for sl3 in range(3):
                    s_ = new_slot()
                    wu = wring[:, s_, :].rearrange("p (k n) -> p k n", k=KC)
                    c0 = sl3 * 1024
                    P.add("pool", lambda h, wu=wu, c0=c0: h.dma_start(out=wu[:, :, :], in_=wuv[:, :, c0:c0 + 1024]),
                          writes=[("w", s_, 0), ("w", s_, 1)], dkey=f"w{s_}_0")
                    for jj in range(8):
                        j = sl3 * 8 + jj
                        gg = j // 3
                        cb = j // 4
                        jo = (j % 4) * 128
                        for ti, tile in enumerate(tiles):
                            n = tile[2]
                            l0 = 512 * ti
                            bu_ = new_bank()
                            for kc in range(KC):
                                P.add("pe", lambda h, bu_=bu_, kc=kc, jj=jj, wu=wu, l0=l0, n=n: h.matmul(
                                    banks[bu_][:, :n], lhsT=wu[:, kc, jj * 128:(jj + 1) * 128], rhs=hq[:, kc, l0:l0 + n],
                                    start=(kc == 0), stop=(kc == KC - 1)),
                                    reads=[("w", s_, 0), ("hq", kc, ti)], writes=[("ps", bu_)])
                            tk = ti
                            P.add("act", lambda h, bu_=bu_, n=n, tk=tk: h.activation(out=tmp[:, tk, :n], in_=banks[bu_][:, :n], func=AF.Gelu_apprx_tanh),
                                  reads=[("ps", bu_)], writes=[("tmp", tk)])
                            bm = new_bank()
                            if ti == 0:
                                for c4 in range(4):
                                    P.add("pe", lambda h, bm=bm, c4=c4, cb=cb, jo=jo, gg=gg: h.matmul(
                                        banks[bm][:, c4 * 128:(c4 + 1) * 128], lhsT=vb[:, c4, cb * 512 + jo:cb * 512 + jo + 128],
                                        rhs=wmb[:, gg, :], start=True, stop=True),
                                        reads=[("vn", c4, cb), "wmb"], writes=[("ps", bm)])
                                P.add("dve", lambda h, bm=bm, gg=gg: h.tensor_tensor(
                                    out=tmp2[:, :].rearrange("p (a c) -> p a c", a=4),
                                    in0=banks[bm][:, :].rearrange("p (a c) -> p a c", a=4),
                                    in1=bsb[:, gg, :].unsqueeze(1).broadcast_to([128, 4, 128]), op=ALU.add),
                                    reads=[("ps", bm), "bsb"], writes=["tmp2"])
                            else:
                                P.add("pe", lambda h, bm=bm, cb=cb, jo=jo, gg=gg: h.matmul(
                                    banks[bm][:, :NS], lhsT=vsb[:NS, cb * 512 + jo:cb * 512 + jo + 128],
                                    rhs=wsb[:NS, gg, :], start=True, stop=True),
                                    reads=[("vn", "s", cb), "wsb"], writes=[("ps", bm)])
                                P.add("dve", lambda h, bm=bm, gg=gg: h.tensor_tensor(
                                    out=tmp2[:, :NS], in0=banks[bm][:, :NS], in1=bss[:, gg, :], op=ALU.add),
                                    reads=[("ps", bm), "bss"], writes=["tmp2"])
                            P.add("dve", lambda h, j=j, l0=l0, n=n, tk=tk: h.tensor_tensor(
                                out=gT[:, j, l0:l0 + n], in0=tmp[:, tk, :n], in1=tmp2[:, :n], op=ALU.mult),
                                reads=[("tmp", tk), "tmp2"], writes=[("gT", j, ti)])
                for n0 in range(0, KC, 2):
                    s_ = new_slot()
                    wo_ = wring[:, s_, :24 * 256].rearrange("p (k n) -> p k n", k=24)
                    P.add("pool", lambda h, wo_=wo_, n0=n0: h.dma_start(out=wo_[:, :, :], in_=wob[:, :, n0 * 128:(n0 + 2) * 128]),
                          writes=[("w", s_, 0), ("w", s_, 1)], dkey=f"w{s_}_0")
                    for ti, tile in enumerate(tiles):
                        tname, x0, n = tile
                        l0 = 512 * ti
                        for nn in range(2):
                            b = new_bank()
                            for kc in range(24):
                                P.add("pe", lambda h, b=b, kc=kc, nn=nn, wo_=wo_, l0=l0, n=n: h.matmul(
                                    banks[b][:, :n], lhsT=wo_[:, kc, nn * 128:(nn + 1) * 128], rhs=gT[:, kc, l0:l0 + n],
                                    start=(kc == 0), stop=(kc == 23)),
                                    reads=[("w", s_, 0), ("gT", kc, ti)], writes=[("ps", b)])
                            kco = n0 + nn
                            P.add("dve", lambda h, b=b, kco=kco, x0=x0, n=n: h.tensor_tensor(
                                out=xT[:, kco, x0:x0 + n], in0=xT[:, kco, x0:x0 + n], in1=banks[b][:, :n], op=ALU.add),
                                reads=[("ps", b)] + xT_res(kco, tname), writes=xT_res(kco, tname))

        state["xs_out"] = 0

        phase_begin()
        xs = alloc([2, D], F32)
        load_x(xs)
        need_layout = True
        for L in range(depth):
            for which in (1, 2):
                for hf in range(2):
                    if need_layout:
                        phase_begin()
                        xn = alloc([KC, 1040], BF16)
                        act = alloc([NJ, 1040], BF16)
                        sq = alloc([KC, 512], BF16)
                        rstd = alloc([2, 512], F32)
                        tmp = alloc([3, 512], F32)
                        need_layout = False
                    ffn(L, which, hf, xn, act, sq, rstd, tmp)
                if which == 1 and cfg.get("mixers", True):
                    if L % 2 == 0:
                        mixer_a(L)
                    else:
                        mixer_b(L)
                    need_layout = True
        phase_begin()
        xs = alloc([2, D], F32)
        yT = alloc([KC, 512], F32)
        sq = alloc([KC, 512], BF16)
        rstd = alloc([2, 512], F32)
        final_out(xs, yT, sq, rstd)

        for op in P.ops:
            if op.dkey is not None and op.dkey not in dsems:
                dsems[op.dkey] = es.enter_context(nc.semaphore("d_" + op.dkey))
        run = P.emit(nc, None, sems, dsems)
        last_out = {}
        for op in P.ops:
            if op.dkey is not None and op.eng == "sp":
                last_out[op.dkey] = max(op.val, last_out.get(op.dkey, 0))
        with nc.Block() as block:
            @block.tensor
            def _(h):
                run("pe", h)

            @block.scalar
            def _(h):
                run("act", h)

            @block.vector
            def _(h):
                run("dve", h)

            @block.gpsimd
            def _(h):
                run("pool", h)

            @block.sync
            def _(h):
                run("sp", h)
                for k, v in last_out.items():
                    h.wait_ge(dsems[k], v)
    return nc


def make_inputs(cfg, inputs):
    f = lambda k: np.asarray(inputs[k], dtype=np.float32)
    xp, xs_ = f("x_prompt"), f("x_sample")
    kq = np.arange(128)
    prev = (kq[:, None] >= kq[None, :]).astype(np.float32)
    own = (kq[:, None] <= kq[None, :]).astype(np.float32)
    NEGM = -30000.0
    prevb = np.where(prev > 0, 0.0, NEGM).astype(np.float32)
    ownb = np.where(own > 0, 0.0, NEGM).astype(np.float32)
    allneg = np.full_like(prevb, NEGM)
    maskN = np.concatenate([prevb, ownb, prevb, ownb], axis=1)
    onesel = np.zeros((128, 256), np.float32)
    for tok in range(16):
        onesel[:, tok * 16 + tok] = 1.0
    bdm = np.zeros((8, 512), np.float32)
    for h in range(8):
        bdm[h, h * 64:(h + 1) * 64] = 1.0
    NEG = -30000.0
    mbc = np.where(kq[:, None] >= np.arange(4)[None, :], 0.0, NEG).astype(np.float32)
    mbn = np.full((16, 48), NEG, np.float32)
    for tok in range(16):
        b, t = tok // 4, tok % 4
        for tok2 in range(16):
            b2, t2 = tok2 // 4, tok2 % 4
            if b2 == b and t2 <= t:
                mbn[tok2, 0 * 16 + tok] = 0.0
        mbn[tok, 16 + tok] = 0.0
        mbn[tok, 32 + tok] = 0.0
    wsp = f("w_spatial")
    wspT = np.ascontiguousarray(wsp.transpose(0, 1, 3, 2))
    wsmT = np.zeros((2, 8, 16, 16), np.float32)
    trilS = np.zeros((16, 16), np.float32)
    for b in range(4):
        wsmT[:, :, 4 * b:4 * b + 4, 4 * b:4 * b + 4] = wspT[:, :, :4, :4]
        for s_ in range(4):
            for t_ in range(s_, 4):
                trilS[4 * b + s_, 4 * b + t_] = 1.0
    bsp = f("b_spatial")
    bsm = np.ascontiguousarray(np.tile(bsp[:, :, :4], (1, 1, 4)))
    trilT = own.copy()
    shared = {
        "ident": np.eye(128, dtype=np.float32),
        "norm_ffn1": f("norm_ffn1"), "norm_ffn2": f("norm_ffn2"), "norm_mix": f("norm_mix"),
        "norm_final": f("norm_final").reshape(1, D),
        "w_ffn1_in": f("w_ffn1_in"), "w_ffn1_out": f("w_ffn1_out"),
        "w_ffn2_in": f("w_ffn2_in"), "w_ffn2_out": f("w_ffn2_out"),
        "w_qkv_a": f("w_qkv_a"), "w_out_a": f("w_out_a"),
        "maskN": maskN, "onesel": onesel, "bdm": bdm, "mbc": mbc, "mbn": mbn,
        "w_uv_b": f("w_uv_b"), "w_out_b": f("w_out_b"), "ln_v_gain": f("ln_v_gain"), "ln_v_bias": f("ln_v_bias"),
        "wspT": wspT, "wsmT": wsmT, "b_spatial": bsp, "bsm": bsm, "trilT": trilT, "trilS": trilS,
    }
    c128, c512, c2048 = f("cache_kv_w128"), f("cache_kv_w512"), f("cache_kv_w2048")
    maps = []
    for c in range(8):
        p, hf = c // 2, c % 2
        m = dict(shared)
        m["xin"] = np.ascontiguousarray(
            np.concatenate([xp[p, hf * TP:(hf + 1) * TP], xs_[4 * c:4 * c + 4].reshape(NS, D)], axis=0))
        m["maskH"] = maskN if hf == 1 else np.concatenate([allneg, ownb, allneg, ownb], axis=1)
        m["cache128"] = np.ascontiguousarray(c128[:, 4 * c:4 * c + 4].reshape(2, 4, 128, 1024))
        m["cache512"] = np.ascontiguousarray(c512[:, 4 * c:4 * c + 4].reshape(2, 4, 512, 1024))
        m["cache2048"] = np.ascontiguousarray(c2048[:, 4 * c:4 * c + 4].reshape(2, 4, 2048, 1024))
        maps.append(m)
    return maps


_NC_CACHE = {}


def run_cfg(cfg, inputs, trace=False):
    key = tuple(sorted(cfg.items()))
    if key not in _NC_CACHE:
        _NC_CACHE[key] = build_nc(cfg)
    nc = _NC_CACHE[key]
    maps = make_inputs(cfg, inputs)
    names = set()
    for alloc in nc.allocations:
        if isinstance(alloc, mybir.MemoryLocationSet) and alloc.kind == "ExternalInput":
            names.add(alloc.memorylocations[0].name)
    maps = [{k: v for k, v in m.items() if k in names} for m in maps]
    return run_bass_kernel_spmd(nc, maps, core_ids=list(range(8)), trace=trace)


def assemble(res):
    yp = np.zeros((4, 4096, D), np.float32)
    ys = np.zeros((32, 4, D), np.float32)
    kvp = [np.zeros((2, 4, w, 2, 8, 64), np.float32) for w in (128, 512, 2048)]
    kvs = [np.zeros((2, 32, 4, 2, 8, 64), np.float32) for _ in range(3)]
    vch = np.zeros((2, 32, 4, DV), np.float32)
    for c in range(8):
        p, hf = c // 2, c % 2
        r = res.results[c]
        y = r["yout"]
        yp[p, hf * TP:(hf + 1) * TP] = y[:TP]
        ys[4 * c:4 * c + 4] = y[TP:].reshape(4, 4, D)
        if hf == 1:
            for g, nm in enumerate(("kvp128", "kvp512", "kvp2048")):
                kvp[g][:, p] = r[nm].reshape(2, -1, 2, 8, 64)
        for g in range(3):
            kvs[g][:, 4 * c:4 * c + 4] = r["kvs"][:, g].reshape(2, 4, 4, 2, 8, 64)
        vch[:, 4 * c:4 * c + 4] = r["vchunk"].reshape(2, 4, 4, DV)
    return (yp, ys, kvp[0], kvp[1], kvp[2], kvs[0], kvs[1], kvs[2], vch)


def kernel(**inputs):
    res = run_cfg({}, inputs)
    return assemble(res)
```
